# Optimizing a Trainium2 kernel written in Bass

```python
import math
import jax, jax.numpy as jnp
from jax import lax
import numpy as np

D_MODEL = 1024
BATCH = 4
SEQ = 4096
DEPTH = 4

GRID_W = 64
CTX_LEN = 256
N_EVEN = (DEPTH + 1) // 2
N_ODD = DEPTH // 2

D_HY = D_MODEL // 2
HY_ORDER = 2
HY_SHORT = 3
HY_EMB_BANDS = 16
HY_EMB = 1 + 2 * HY_EMB_BANDS
HY_FILTER_HIDDEN = 64
HY_DECAY_TARGET = 1e-2
HY_FAST_DECAY = 0.3
HY_SLOW_DECAY = 1.5
HEAD_DIM = 64
N_HEADS = (D_MODEL // 2) // HEAD_DIM
N_KV_HEADS = 2
GROUP = N_HEADS // N_KV_HEADS
WINDOW = 128
ATT_BLOCK = 128
ROPE_BASE = 10000.0
HY_COLS = 3 * D_HY
Q_COLS = N_HEADS * HEAD_DIM
KV_COLS = N_KV_HEADS * HEAD_DIM
IN_COLS = HY_COLS + Q_COLS + 2 * KV_COLS
MIX_OUT = D_HY + Q_COLS
D_CONF = D_MODEL
CONF_KERNEL = 31
N_EXPERTS = 32
TOP_K = 4
D_EXPERT = D_MODEL
SWIGLU_LIMIT = 7.0
SWIGLU_ALPHA = 1.702
MOE_BLOCK = 128
EPS = 1e-6
NEG_INF = -1e30

kernel_name = 'hybrid_hyena_swa_conformer_moe_dit'


def rms_norm(x, g):
    xf = x.astype(jnp.float32)
    y = xf * lax.rsqrt(jnp.mean(xf * xf, axis=-1, keepdims=True) + EPS)
    return (y * g.astype(jnp.float32)).astype(x.dtype)


def layer_norm(x, g, b):
    xf = x.astype(jnp.float32)
    mu = jnp.mean(xf, axis=-1, keepdims=True)
    var = jnp.mean(jnp.square(xf - mu), axis=-1, keepdims=True)
    y = (xf - mu) * lax.rsqrt(var + EPS)
    return (y * g.astype(jnp.float32) + b.astype(jnp.float32)).astype(x.dtype)


def adaln(c_vec, w, b, n):
    m = jax.nn.silu(c_vec) @ w + b
    return [t[:, None, :] for t in jnp.split(m, n, axis=-1)]


def modulate(h, shift, scale):
    return h * (1 + scale) + shift


def depthwise_conv(x, w, b):
    k = w.shape[0]
    pad = (k - 1) // 2
    y = lax.conv_general_dilated(x, w[:, None, :].astype(x.dtype), window_strides=(1,),
                                 padding=[(pad, pad)], dimension_numbers=('NWC', 'WIO', 'NWC'),
                                 feature_group_count=x.shape[-1])
    return y + b


def hyena_filters(L, w1, b1, w2, b2, w3, b3, w_out, freq):
    f32 = jnp.float32
    t = jnp.arange(L, dtype=f32)
    t_norm = t / max(L - 1, 1)
    bands = jnp.linspace(1e-4, HY_EMB_BANDS - 1, HY_EMB_BANDS, dtype=f32)
    ang = (2.0 * math.pi / L) * t[:, None] * bands[None, :]
    feats = jnp.concatenate([t_norm[:, None], jnp.cos(ang), jnp.sin(ang)], axis=-1)
    fr = freq.astype(f32)
    h = jnp.sin(fr * (feats @ w1.astype(f32) + b1.astype(f32)))
    h = jnp.sin(fr * (h @ w2.astype(f32) + b2.astype(f32)))
    h = jnp.sin(fr * (h @ w3.astype(f32) + b3.astype(f32)))
    h = (h @ w_out.astype(f32)).reshape(L, 2, HY_ORDER, D_HY)
    log_t = math.log(HY_DECAY_TARGET)
    deltas = jnp.abs(jnp.linspace(log_t / HY_SLOW_DECAY, log_t / HY_FAST_DECAY, D_HY, dtype=f32))
    h = h * jnp.exp(-t_norm[:, None] * deltas[None, :])[:, None, None, :]
    fwd, bwd = h[:, 0], h[:, 1]
    buf = jnp.concatenate([fwd, jnp.zeros((1, HY_ORDER, D_HY), f32), bwd[:0:-1]], axis=0)
    buf = buf * lax.rsqrt(jnp.sum(buf * buf, axis=0, keepdims=True) + EPS)
    return jnp.fft.rfft(buf, axis=0)


def long_conv(z, h_freq):
    L = z.shape[1]
    zf = jnp.fft.rfft(z.astype(jnp.float32), n=2 * L, axis=1)
    y = jnp.fft.irfft(zf * h_freq[None], n=2 * L, axis=1)[:, :L]
    return y.astype(z.dtype)


def hyena(u, conv_w, conv_b, h_freq, skip):
    u = depthwise_conv(u, conv_w, conv_b)
    v, x1, x2 = jnp.split(u, 3, axis=-1)
    z = v
    for n, gate in enumerate((x1, x2)):
        z = gate * (long_conv(z, h_freq[:, n]) + skip[n] * z)
    return z


def rope_1d(x, pos):
    n = x.shape[-1] // 2
    inv = ROPE_BASE ** (-jnp.arange(n, dtype=jnp.float32) / n)
    ang = pos[:, None] * inv[None, :]
    cos = jnp.cos(ang)[:, None, :].astype(x.dtype)
    sin = jnp.sin(ang)[:, None, :].astype(x.dtype)
    x1, x2 = x[..., :n], x[..., n:]
    return jnp.concatenate([x1 * cos - x2 * sin, x2 * cos + x1 * sin], axis=-1)


def rope_2d(x, row_pos, col_pos):
    half = x.shape[-1] // 2
    return jnp.concatenate([rope_1d(x[..., :half], row_pos), rope_1d(x[..., half:], col_pos)], axis=-1)


def windowed_attention(q, k, v, k_ctx, v_ctx, sink):
    B, L = q.shape[:2]
    nb = L // ATT_BLOCK
    kw_len = 3 * ATT_BLOCK
    qb = q.reshape(B, nb, ATT_BLOCK, N_KV_HEADS, GROUP, HEAD_DIM)
    pad = ((0, 0), (ATT_BLOCK, ATT_BLOCK), (0, 0), (0, 0))
    kp = jnp.pad(k, pad).reshape(B, nb + 2, ATT_BLOCK, N_KV_HEADS, HEAD_DIM)
    vp = jnp.pad(v, pad).reshape(B, nb + 2, ATT_BLOCK, N_KV_HEADS, HEAD_DIM)
    kw = jnp.concatenate([kp[:, :-2], kp[:, 1:-1], kp[:, 2:]], axis=2)
    vw = jnp.concatenate([vp[:, :-2], vp[:, 1:-1], vp[:, 2:]], axis=2)
    r = jnp.arange(ATT_BLOCK)[:, None]
    j = jnp.arange(kw_len)[None, :]
    kpos = jnp.arange(nb)[:, None, None] * ATT_BLOCK - ATT_BLOCK + j[None]
    mask = (jnp.abs(j - ATT_BLOCK - r) <= WINDOW)[None] & (kpos >= 0) & (kpos < L)
    s_win = jnp.einsum('bnqhgd,bnkhd->bnhgqk', qb, kw).astype(jnp.float32)
    s_win = jnp.where(mask[None, :, None, None], s_win, NEG_INF)
    s_ctx = jnp.einsum('bnqhgd,bkhd->bnhgqk', qb, k_ctx).astype(jnp.float32)
    s_sink = jnp.broadcast_to(sink.astype(jnp.float32)[None, None, :, :, None, None], s_win.shape[:-1] + (1,))
    p = jax.nn.softmax(jnp.concatenate([s_win, s_ctx, s_sink], axis=-1), axis=-1)
    n_ctx = k_ctx.shape[1]
    out = (jnp.einsum('bnhgqk,bnkhd->bnqhgd', p[..., :kw_len].astype(v.dtype), vw)
           + jnp.einsum('bnhgqk,bkhd->bnqhgd', p[..., kw_len:kw_len + n_ctx].astype(v.dtype), v_ctx))
    return out.reshape(B, L, N_HEADS * HEAD_DIM)


def context_attention(q, k, v, sink):
    B, C = q.shape[:2]
    qg = q.reshape(B, C, N_KV_HEADS, GROUP, HEAD_DIM)
    s = jnp.einsum('bqhgd,bkhd->bhgqk', qg, k).astype(jnp.float32)
    s_sink = jnp.broadcast_to(sink.astype(jnp.float32)[None, :, :, None, None], s.shape[:-1] + (1,))
    p = jax.nn.softmax(jnp.concatenate([s, s_sink], axis=-1), axis=-1)[..., :-1]
    out = jnp.einsum('bhgqk,bkhd->bqhgd', p.astype(v.dtype), v)
    return out.reshape(B, C, N_HEADS * HEAD_DIM)


def conformer_conv(h, w1, b1, dw_w, dw_b, ln_g, ln_b, w2, b2):
    a = h @ w1 + b1
    a, g = jnp.split(a, 2, axis=-1)
    a = a * jax.nn.sigmoid(g)
    a = depthwise_conv(a, dw_w, dw_b)
    a = jax.nn.silu(layer_norm(a, ln_g, ln_b))
    return a @ w2 + b2


def moe(h, rw, rb, wg, bg, wu, bu, wd, bd):
    B, L, D = h.shape
    T = B * L
    TK = T * TOP_K
    xf = h.reshape(T, D)
    logits = (xf @ rw + rb).astype(jnp.float32)
    top_v, top_i = lax.top_k(logits, TOP_K)
    gates = jax.nn.softmax(top_v, axis=-1)
    eid = top_i.reshape(-1)
    tok = jnp.arange(TK, dtype=jnp.int32) // TOP_K
    gw = gates.reshape(-1)
    order = jnp.argsort(eid)
    se, st, sg = eid[order], tok[order], gw[order]
    counts = jnp.bincount(eid, length=N_EXPERTS)
    starts = jnp.cumsum(counts) - counts
    pcounts = (counts + MOE_BLOCK - 1) // MOE_BLOCK * MOE_BLOCK
    pends = jnp.cumsum(pcounts)
    pstarts = pends - pcounts
    dest = pstarts[se] + (jnp.arange(TK, dtype=jnp.int32) - starts[se])
    n_rows = TK + N_EXPERTS * MOE_BLOCK
    n_blk = n_rows // MOE_BLOCK
    row_tok = jnp.full((n_rows,), T, dtype=jnp.int32).at[dest].set(st)
    row_gate = jnp.zeros((n_rows,), jnp.float32).at[dest].set(sg)
    blk_exp = jnp.minimum(jnp.searchsorted(pends, jnp.arange(n_blk, dtype=jnp.int32) * MOE_BLOCK, side='right'),
                          N_EXPERTS - 1)
    x_rows = jnp.concatenate([xf, jnp.zeros((1, D), xf.dtype)], axis=0)[row_tok].reshape(n_blk, MOE_BLOCK, D)

    def expert_block(args):
        xb, e = args
        g = jnp.minimum(xb @ wg[e] + bg[e], SWIGLU_LIMIT)
        u = jnp.clip(xb @ wu[e] + bu[e], -SWIGLU_LIMIT, SWIGLU_LIMIT)
        a = g * jax.nn.sigmoid(SWIGLU_ALPHA * g) * (u + 1)
        return a @ wd[e] + bd[e]

    y_rows = lax.map(expert_block, (x_rows, blk_exp)).reshape(n_rows, D)
    y = jnp.zeros((T + 1, D), h.dtype).at[row_tok].add(y_rows * row_gate[:, None].astype(h.dtype))
    return y[:T].reshape(B, L, D)


def setup_inputs(seed: int = 0) -> dict:
    key = jax.random.key(seed)
    ks = iter(jax.random.split(key, 64))
    f32 = jnp.float32
    D = D_MODEL

    def nrm(shape, scale):
        return jax.random.normal(next(ks), shape, f32) * scale

    def gain(shape):
        return 1.0 + nrm(shape, 0.05)

    return {
        'x': nrm((BATCH, SEQ, D), 1.0),
        'c': nrm((BATCH, D), 1.0),
        'ctx': nrm((BATCH, CTX_LEN, D), 1.0),
        'c_ctx': nrm((D,), 1.0),
        'mod_w': nrm((DEPTH, D, 6 * D), D ** -0.5),
        'mod_b': nrm((DEPTH, 6 * D), 0.01),
        'norm1_g': gain((DEPTH, D)),
        'norm2_g': gain((DEPTH, D)),
        'ev_w_in': nrm((N_EVEN, D, IN_COLS), D ** -0.5),
        'ev_w_out': nrm((N_EVEN, MIX_OUT, D), MIX_OUT ** -0.5),
        'hy_conv_w': nrm((N_EVEN, HY_SHORT, HY_COLS), HY_SHORT ** -0.5),
        'hy_conv_b': nrm((N_EVEN, HY_COLS), 0.01),
        'hy_w1': nrm((N_EVEN, HY_EMB, HY_FILTER_HIDDEN), HY_EMB ** -0.5),
        'hy_b1': nrm((N_EVEN, HY_FILTER_HIDDEN), 0.1),
        'hy_w2': nrm((N_EVEN, HY_FILTER_HIDDEN, HY_FILTER_HIDDEN), HY_FILTER_HIDDEN ** -0.5),
        'hy_b2': nrm((N_EVEN, HY_FILTER_HIDDEN), 0.1),
        'hy_w3': nrm((N_EVEN, HY_FILTER_HIDDEN, HY_FILTER_HIDDEN), HY_FILTER_HIDDEN ** -0.5),
        'hy_b3': nrm((N_EVEN, HY_FILTER_HIDDEN), 0.1),
        'hy_w_out': nrm((N_EVEN, HY_FILTER_HIDDEN, 2 * HY_ORDER * D_HY), HY_FILTER_HIDDEN ** -0.5),
        'hy_freq': gain((N_EVEN, HY_FILTER_HIDDEN)),
        'hy_skip': nrm((N_EVEN, HY_ORDER, D_HY), 0.5),
        'q_norm_g': gain((N_EVEN, HEAD_DIM)),
        'k_norm_g': gain((N_EVEN, HEAD_DIM)),
        'attn_sink': nrm((N_EVEN, N_HEADS), 0.5),
        'cf_w1': nrm((N_ODD, D, 2 * D_CONF), D ** -0.5),
        'cf_b1': nrm((N_ODD, 2 * D_CONF), 0.01),
        'cf_dw_w': nrm((N_ODD, CONF_KERNEL, D_CONF), CONF_KERNEL ** -0.5),
        'cf_dw_b': nrm((N_ODD, D_CONF), 0.01),
        'cf_ln_g': gain((N_ODD, D_CONF)),
        'cf_ln_b': nrm((N_ODD, D_CONF), 0.01),
        'cf_w2': nrm((N_ODD, D_CONF, D), D_CONF ** -0.5),
        'cf_b2': nrm((N_ODD, D), 0.01),
        'moe_router_w': nrm((DEPTH, D, N_EXPERTS), D ** -0.5),
        'moe_router_b': nrm((DEPTH, N_EXPERTS), 0.01),
        'moe_w_gate': nrm((DEPTH, N_EXPERTS, D, D_EXPERT), D ** -0.5),
        'moe_b_gate': nrm((DEPTH, N_EXPERTS, D_EXPERT), 0.01),
        'moe_w_up': nrm((DEPTH, N_EXPERTS, D, D_EXPERT), D ** -0.5),
        'moe_b_up': nrm((DEPTH, N_EXPERTS, D_EXPERT), 0.01),
        'moe_w_down': nrm((DEPTH, N_EXPERTS, D_EXPERT, D), D_EXPERT ** -0.5),
        'moe_b_down': nrm((DEPTH, N_EXPERTS, D), 0.01),
    }


def reference(x, c, ctx, c_ctx, mod_w, mod_b, norm1_g, norm2_g, ev_w_in, ev_w_out,
              hy_conv_w, hy_conv_b, hy_w1, hy_b1, hy_w2, hy_b2, hy_w3, hy_b3, hy_w_out, hy_freq, hy_skip,
              q_norm_g, k_norm_g, attn_sink,
              cf_w1, cf_b1, cf_dw_w, cf_dw_b, cf_ln_g, cf_ln_b, cf_w2, cf_b2,
              moe_router_w, moe_router_b, moe_w_gate, moe_b_gate, moe_w_up, moe_b_up, moe_w_down, moe_b_down):
    B, L, _ = x.shape
    C = ctx.shape[1]
    ROWS = L // GRID_W
    row_pos = jnp.repeat(jnp.arange(ROWS, dtype=jnp.float32), GRID_W)
    col_pos = jnp.tile(jnp.arange(GRID_W, dtype=jnp.float32), ROWS)
    q_scale = HEAD_DIM ** -0.5
    last_even = (DEPTH - 1) // 2 * 2
    c_con = c_ctx[None, :]
    xl, xc = x, ctx
    for l in range(DEPTH):
        ctx_live = l < last_even
        sh1, sc1, g1, sh2, sc2, g2 = adaln(c, mod_w[l], mod_b[l], 6)
        if ctx_live:
            csh1, csc1, cg1, csh2, csc2, cg2 = adaln(c_con, mod_w[l], mod_b[l], 6)
        if l % 2 == 0:
            e = l // 2
            filt = (hy_w1[e], hy_b1[e], hy_w2[e], hy_b2[e], hy_w3[e], hy_b3[e], hy_w_out[e], hy_freq[e])
            sink = attn_sink[e].reshape(N_KV_HEADS, GROUP)
            h = modulate(rms_norm(xl, norm1_g[l]), sh1, sc1)
            p = h @ ev_w_in[e]
            hy = hyena(p[..., :HY_COLS], hy_conv_w[e], hy_conv_b[e], hyena_filters(L, *filt), hy_skip[e])
            q = p[..., HY_COLS:HY_COLS + Q_COLS].reshape(B, L, N_HEADS, HEAD_DIM)
            k = p[..., HY_COLS + Q_COLS:HY_COLS + Q_COLS + KV_COLS].reshape(B, L, N_KV_HEADS, HEAD_DIM)
            v = p[..., HY_COLS + Q_COLS + KV_COLS:].reshape(B, L, N_KV_HEADS, HEAD_DIM)
            q = rope_2d(rms_norm(q, q_norm_g[e]), row_pos, col_pos) * q_scale
            k = rope_2d(rms_norm(k, k_norm_g[e]), row_pos, col_pos)
            if not ctx_live:
                csh1, csc1 = adaln(c_con, mod_w[l][:, :2 * D_MODEL], mod_b[l][:2 * D_MODEL], 2)
            hc = modulate(rms_norm(xc, norm1_g[l]), csh1, csc1)
            pc = hc @ ev_w_in[e] if ctx_live else hc @ ev_w_in[e][:, HY_COLS + Q_COLS:]
            k_c = rms_norm(pc[..., -2 * KV_COLS:-KV_COLS].reshape(B, C, N_KV_HEADS, HEAD_DIM), k_norm_g[e])
            v_c = pc[..., -KV_COLS:].reshape(B, C, N_KV_HEADS, HEAD_DIM)
            att = windowed_attention(q, k, v, k_c, v_c, sink)
            xl = xl + g1 * (jnp.concatenate([hy, att], axis=-1) @ ev_w_out[e])
            if ctx_live:
                hy_c = hyena(pc[..., :HY_COLS], hy_conv_w[e], hy_conv_b[e], hyena_filters(C, *filt), hy_skip[e])
                q_c = rms_norm(pc[..., HY_COLS:HY_COLS + Q_COLS].reshape(B, C, N_HEADS, HEAD_DIM), q_norm_g[e]) * q_scale
                att_c = context_attention(q_c, k_c, v_c, sink)
                xc = xc + cg1 * (jnp.concatenate([hy_c, att_c], axis=-1) @ ev_w_out[e])
        else:
            o = l // 2
            cf = (cf_w1[o], cf_b1[o], cf_dw_w[o], cf_dw_b[o], cf_ln_g[o], cf_ln_b[o], cf_w2[o], cf_b2[o])
            xl = xl + g1 * conformer_conv(modulate(rms_norm(xl, norm1_g[l]), sh1, sc1), *cf)
            if ctx_live:
                xc = xc + cg1 * conformer_conv(modulate(rms_norm(xc, norm1_g[l]), csh1, csc1), *cf)
        mw = (moe_router_w[l], moe_router_b[l], moe_w_gate[l], moe_b_gate[l], moe_w_up[l], moe_b_up[l],
              moe_w_down[l], moe_b_down[l])
        xl = xl + g2 * moe(modulate(rms_norm(xl, norm2_g[l]), sh2, sc2), *mw)
        if ctx_live:
            xc = xc + cg2 * moe(modulate(rms_norm(xc, norm2_g[l]), csh2, csc2), *mw)
    return xl
```

```python
import numpy as np
import ml_dtypes
import concourse.bass as bass
import concourse.mybir as mybir
from concourse.bass_utils import run_bass_kernel_spmd
from contextlib import ExitStack

F32 = mybir.dt.float32
BF16 = mybir.dt.bfloat16
I32 = mybir.dt.int32
AF = mybir.ActivationFunctionType
ALU = mybir.AluOpType
AX = mybir.AxisListType

NDMA = 12
EPS = 1e-6
D = 1024
KB = 8
NCORES = 8


class Prog:
    ENG = ("pe", "act", "dve", "pool", "sp")

    def __init__(self):
        self.nc = bass.Bass("TRN2", target_bir_lowering=False)
        self.es = ExitStack()
        self.q = {e: [] for e in self.ENG}
        self.cnt = {}
        self.lastw = {}
        self.readers = {}
        self.waited = {e: {} for e in self.ENG}
        self.dma_i = 0
        self.sems = {}
        self.n_t = 0

    def dram_in(self, name, shape, dt=F32):
        return self.nc.dram_tensor(name, list(shape), dt, kind="ExternalInput").ap()

    def dram_out(self, name, shape, dt=F32):
        return self.nc.dram_tensor(name, list(shape), dt, kind="ExternalOutput").ap()

    def dram_tmp(self, name, shape, dt=F32):
        return self.nc.dram_tensor(name, list(shape), dt, kind="Internal").ap()

    def sb(self, shape, dt=F32, name=None):
        self.n_t += 1
        return self.es.enter_context(self.nc.sbuf_tensor(name or f"sb{self.n_t}", list(shape), dt))

    def ps(self, shape=(128, 512), dt=F32, name=None):
        self.n_t += 1
        return self.es.enter_context(self.nc.psum_tensor(name or f"ps{self.n_t}", list(shape), dt))

    def _deps(self, eng, reads, writes):
        deps = {}

        def add(d):
            if deps.get(d[0], 0) < d[1]:
                deps[d[0]] = d[1]
        for k in reads:
            if k in self.lastw:
                add(self.lastw[k])
        for k in writes:
            if k in self.lastw:
                add(self.lastw[k])
            for r in self.readers.get(k, ()):
                add(r)
        w = self.waited[eng]
        out = []
        for se, si in deps.items():
            if w.get(se, 0) < si:
                w[se] = si
                out.append((se, si))
        return out

    def _record(self, tag, idx, reads, writes):
        for k in writes:
            self.lastw[k] = (tag, idx)
            self.readers[k] = []
        for k in reads:
            self.readers.setdefault(k, []).append((tag, idx))

    def op(self, eng, fn, reads=(), writes=(), accum=False):
        reads, writes = tuple(reads), tuple(writes)
        waits = self._deps(eng, reads, writes)
        if accum:
            waits = [w for w in waits if w[0] != eng]
        self.cnt[eng] = self.cnt.get(eng, 0) + 1
        self.q[eng].append((waits, fn, eng, 1))
        self._record(eng, self.cnt[eng], reads, writes)

    def dma(self, out, in_, reads=(), writes=(), queue="sp", **kw):
        reads, writes = tuple(reads), tuple(writes)
        self.dma_n = getattr(self, "dma_n", {})
        i_ = self.dma_n.get(queue, 0)
        self.dma_n[queue] = i_ + 1
        slot = ("d" if queue == "sp" else "g", i_ % NDMA)
        waits = self._deps(queue, reads, writes)
        prev = self.cnt.get(slot, 0)
        if prev and self.waited[queue].get(slot, 0) < prev:
            self.waited[queue][slot] = prev
            waits.append((slot, prev))
        self.cnt[slot] = prev + 1

        def fn(e, out=out, in_=in_, kw=kw):
            return e.dma_start(out=out, in_=in_, **kw)
        self.q[queue].append((waits, fn, slot, 16))
        self._record(slot, prev + 1, reads, writes)

    def barrier(self):
        for eng in self.ENG:
            waits = []
            for tag, c in self.cnt.items():
                if tag != eng and c and self.waited[eng].get(tag, 0) < c:
                    self.waited[eng][tag] = c
                    waits.append((tag, c))
            if waits:
                self.q[eng].append((waits, None, None, 0))
        self.lastw = {}
        self.readers = {}

    EPOCH = 8192

    def _sem(self, tag, idx=None):
        if isinstance(tag, str):
            ep = (idx - 1) // self.EPOCH
            key = (tag, ep)
            nm = f"s_{tag}{ep}"
        else:
            key = tag
            nm = f"s_{tag[0]}{tag[1]}"
        if key not in self.sems:
            self.sems[key] = self.es.enter_context(self.nc.semaphore(nm))
        return self.sems[key]

    def _semval(self, tag, idx):
        if isinstance(tag, str):
            return idx - ((idx - 1) // self.EPOCH) * self.EPOCH
        return idx * 16

    def finish(self):
        final = [(tag, c) for tag, c in self.cnt.items() if c and self.waited["sp"].get(tag, 0) < c]
        for tag, c in list(self.cnt.items()):
            if isinstance(tag, str):
                for ep in range((c - 1) // self.EPOCH + 1):
                    self._sem(tag, ep * self.EPOCH + 1)
            else:
                self._sem(tag)
        block = self.es.enter_context(self.nc.Block())
        emap = {"pe": block.tensor, "act": block.scalar, "dve": block.vector,
                "pool": block.gpsimd, "sp": block.sync}
        for eng in self.ENG:
            items = self.q[eng]
            if eng == "sp":
                items = items + [(final, None, None, 0)]
            if not items:
                continue

            def body(e, items=items, eng=eng):
                n_ = 0
                for waits, fn, tag, inc in items:
                    for (se, si) in waits:
                        e.wait_ge(self._sem(se, si), self._semval(se, si))
                    if fn is not None:
                        if isinstance(tag, str):
                            n_ += 1
                            fn(e).then_inc(self._sem(tag, n_), inc)
                        else:
                            fn(e).then_inc(self._sem(tag), inc)
            emap[eng](body)
        self.es.close()
        return self.nc


def chunks_of(T, n=512):
    out = []
    c = 0
    while c < T:
        out.append((c, min(n, T - c)))
        c += n
    return out


class Rot:
    def __init__(self, P, n, shape, dt=F32, psum=False, tag="r"):
        self.t = [(P.ps(shape, dt) if psum else P.sb(shape, dt)) for _ in range(n)]
        self.k = [(tag, id(self), i) for i in range(n)]
        self.i = 0

    def next(self):
        j = self.i % len(self.t)
        self.i += 1
        return self.t[j], self.k[j]


def adaln_vecs(P, cT_d, modw_d, modb_d, nvec, psA, tag, wbuf=None):
    nb_tot = nvec * 8
    cs = P.sb([128, KB, 2])
    P.dma(cs[:], cT_d.rearrange("(kb p) c -> p kb c", p=128), writes=[tag + "cs"], allow_slow_non_contiguous=True)
    P.op("act", lambda e: e.activation(out=cs[:], in_=cs[:], func=AF.Silu), reads=[tag + "cs"], writes=[tag + "cs"])
    mb = P.sb([128, nb_tot])
    P.dma(mb[:], modb_d.rearrange("(nb p) -> p nb", p=128), writes=[tag + "mb"], allow_slow_non_contiguous=True)
    mv = P.sb([128, nb_tot, 2])
    if wbuf is None:
        wbuf = Rot(P, 2, [128, KB, 512], F32, tag=tag + "w")
    ncol = nvec * 1024
    for c0 in range(0, ncol, 512):
        wt, wk = wbuf.next()
        P.dma(wt[:], modw_d[:, c0:c0 + 512].rearrange("(kb p) n -> p kb n", p=128), writes=[wk])
        pt, pk = psA.next()
        for j in range(4):
            for kb in range(KB):
                P.op("pe", lambda e, pt=pt, wt=wt, j=j, kb=kb: e.matmul(
                    pt[:, 2 * j:2 * j + 2], wt[:, kb, j * 128:(j + 1) * 128], cs[:, kb, :],
                    start=(kb == 0), stop=(kb == KB - 1)),
                    reads=[wk, tag + "cs"], writes=[pk], accum=not (j == 0 and kb == 0))
        nb0 = c0 // 128
        P.op("dve", lambda e, pt=pt, nb0=nb0: e.tensor_tensor(
            out=mv[:, nb0:nb0 + 4, :], in0=pt[:, 0:8].rearrange("p (a b) -> p a b", b=2),
            in1=mb[:, nb0:nb0 + 4].unsqueeze(2).to_broadcast([128, 4, 2]), op=ALU.add),
            reads=[pk, tag + "mb"], writes=[tag + "mv"])
    return mv, tag + "mv"


def build_p1(T, ctx0):
    P = Prog()
    xT = P.dram_in("xT", [D, T])
    cT = P.dram_in("cT", [D, 2])
    modw = P.dram_in("modw", [D, 2048])
    modb = P.dram_in("modb", [2048])
    g1 = P.dram_in("g", [D])
    win = P.dram_in("win", [D, 2304])
    pT = P.dram_out("pT", [2304, T])
    psA = Rot(P, 2, [128, 512], F32, psum=True, tag="psA")
    psB = Rot(P, 4, [128, 512], F32, psum=True, tag="psB")
    xbuf = Rot(P, 2, [128, KB, 512], F32, tag="x")
    mv, mvk = adaln_vecs(P, cT, modw, modb, 2, psA, "ad", wbuf=xbuf)
    gs = P.sb([128, KB])
    P.dma(gs[:], g1.rearrange("(kb p) -> p kb", p=128), writes=["gs"], allow_slow_non_contiguous=True)
    A = P.sb([128, KB, 2])
    P.op("dve", lambda e: e.tensor_scalar(out=A[:], in0=mv[:, 8:16, :], scalar1=1.0, scalar2=None, op0=ALU.add),
         reads=[mvk], writes=["A"])
    P.op("dve", lambda e: e.tensor_tensor(out=A[:], in0=A[:], in1=gs[:].unsqueeze(2).to_broadcast([128, KB, 2]), op=ALU.mult),
         reads=["A", "gs"], writes=["A"])
    ones = P.sb([128, 128])
    P.op("pool", lambda e: e.memset(ones[:], 1.0 / D), writes=["ones"])
    wsb = P.sb([128, KB, 2304], BF16)
    for kb in range(KB):
        P.dma(wsb[:, kb, :], win[kb * 128:(kb + 1) * 128, :], writes=[("win", kb)], queue="pool")
    hT = P.sb([128, KB, T], BF16)
    sqb = Rot(P, 1, [128, KB, 512], F32, tag="sq")
    rsb = Rot(P, 2, [128, 512], F32, tag="rs")
    tmpb = Rot(P, 2, [128, 512], F32, tag="tmp")
    outb = Rot(P, 4, [128, 512], F32, tag="out")
    for (c0, n) in chunks_of(T):
        kind = 1 if c0 >= ctx0 else 0
        xt, xk = xbuf.next()
        P.dma(xt[:, :, :n], xT[:, c0:c0 + n].rearrange("(kb p) t -> p kb t", p=128), writes=[xk])
        sq, sqk = sqb.next()
        P.op("act", lambda e, sq=sq, xt=xt, n=n: e.activation(out=sq[:, :, :n], in_=xt[:, :, :n], func=AF.Square),
             reads=[xk], writes=[sqk])
        pt, pk = psA.next()
        for kb in range(KB):
            P.op("pe", lambda e, pt=pt, sq=sq, kb=kb, n=n: e.matmul(pt[:, :n], ones[:], sq[:, kb, :n], start=(kb == 0), stop=(kb == KB - 1)),
                 reads=[sqk, "ones"], writes=[pk], accum=kb > 0)
        rs, rsk = rsb.next()
        P.op("act", lambda e, rs=rs, pt=pt, n=n: e.activation(out=rs[:, :n], in_=pt[:, :n], func=AF.Sqrt, bias=EPS, scale=1.0),
             reads=[pk], writes=[rsk])
        P.op("dve", lambda e, rs=rs, n=n: e.reciprocal(out=rs[:, :n], in_=rs[:, :n]), reads=[rsk], writes=[rsk])
        for kb in range(KB):
            tp, tk = tmpb.next()
            P.op("pool", lambda e, tp=tp, xt=xt, rs=rs, kb=kb, n=n: e.tensor_tensor(out=tp[:, :n], in0=xt[:, kb, :n], in1=rs[:, :n], op=ALU.mult),
                 reads=[xk, rsk], writes=[tk])
            P.op("dve", lambda e, tp=tp, kb=kb, n=n, c0=c0, kind=kind: e.tensor_scalar(
                out=hT[:, kb, c0:c0 + n], in0=tp[:, :n], scalar1=A[:, kb, kind:kind + 1], scalar2=mv[:, kb, kind:kind + 1],
                op0=ALU.mult, op1=ALU.add), reads=[tk, "A", mvk], writes=[("hT", c0)])
        for nb in range(18):
            pt, pk = psB.next()
            for kb in range(KB):
                P.op("pe", lambda e, pt=pt, nb=nb, kb=kb, n=n, c0=c0: e.matmul(
                    pt[:, :n], wsb[:, kb, nb * 128:(nb + 1) * 128], hT[:, kb, c0:c0 + n], start=(kb == 0), stop=(kb == KB - 1)),
                    reads=[("win", kb), ("hT", c0)], writes=[pk], accum=kb > 0)
            ot, ok = outb.next()
            P.op("act", lambda e, ot=ot, pt=pt, n=n: e.copy(out=ot[:, :n], in_=pt[:, :n]), reads=[pk], writes=[ok])
            P.dma(pT[nb * 128:(nb + 1) * 128, c0:c0 + n], ot[:, :n], reads=[ok], writes=[("pT", nb, c0)])
    return P.finish()


HY_L = 4096
HY_C = 256


def dft_consts(L):
    N = 2 * L
    LT = L // 128
    F = (L + 1 + 127) // 128
    t = np.arange(L, dtype=np.int64)
    f = np.arange(F * 128, dtype=np.int64)
    ang = 2.0 * np.pi * ((t[:, None] * f[None, :]) % N).astype(np.float64) / N
    valid = (f <= L).astype(np.float64)[None, :]
    Wc = np.cos(ang) * valid
    Wsn = -np.sin(ang) * valid
    wf = np.where((f == 0) | (f == L), 1.0, 2.0) * (f <= L)
    Vc = (Wc * wf[None, :] / N).T
    Vs = (Wsn * wf[None, :] / N).T
    bf = ml_dtypes.bfloat16

    def fwd(M):
        return np.ascontiguousarray(M.reshape(LT, 128, F, 128).transpose(2, 1, 0, 3)).astype(bf)

    def inv(M):
        return np.ascontiguousarray(M.reshape(F, 128, LT, 128).transpose(2, 1, 0, 3)).astype(bf)
    return fwd(Wc), fwd(Wsn), inv(Vc), inv(Vs)


def hyena_feats(L):
    t = np.arange(L, dtype=np.float32)
    t_norm = t / max(L - 1, 1)
    bands = np.linspace(1e-4, 15, 16, dtype=np.float32)
    ang = (np.float32(2.0 * np.pi / L) * t[:, None] * bands[None, :]).astype(np.float32)
    feats = np.concatenate([t_norm[:, None], np.cos(ang), np.sin(ang)], axis=-1).astype(np.float32)
    log_t = np.log(1e-2)
    deltas = np.abs(np.linspace(log_t / 1.5, log_t / 0.3, 512, dtype=np.float32))
    decay = np.exp(-t_norm[:, None] * deltas[None, :]).astype(np.float32)
    return np.ascontiguousarray(feats.T), decay


def build_hy():
    P = Prog()
    LTM, FM = HY_L // 128, 33
    seqs = [("l", HY_L), ("c", 256)]
    din = {}
    for s, L in seqs:
        LT, F = L // 128, (L + 1 + 127) // 128
        din[s] = dict(
            phy=P.dram_in("phy_" + s, [L + 2, 768]), feats=P.dram_in("feats_" + s, [33, L]),
            decay=P.dram_in("decay_" + s, [L, 256]),
            Wc=P.dram_in("Wc_" + s, [F, 128, LT, 128], BF16), Ws=P.dram_in("Ws_" + s, [F, 128, LT, 128], BF16),
            Vc=P.dram_in("Vc_" + s, [LT, 128, F, 128], BF16), Vs=P.dram_in("Vs_" + s, [LT, 128, F, 128], BF16),
            out=P.dram_out("hy_" + s, [L, 256]), u=P.dram_tmp("u_" + s, [L, 768]), z1=P.dram_tmp("z1_" + s, [L, 256]))
    cw = P.dram_in("cw", [3, 768]); cb = P.dram_in("cb", [1, 768])
    fw1 = P.dram_in("fw1", [33, 64]); fw2 = P.dram_in("fw2", [64, 64]); fw3 = P.dram_in("fw3", [64, 64])
    fb = P.dram_in("fb", [64, 4])
    fwo = P.dram_in("fwo", [64, 1024])
    skip = P.dram_in("skip", [2, 256])
    m0 = P.dram_in("m0", [128, 1])
    cwb = P.sb([128, 3, 768]); cbb = P.sb([128, 768]); skb = P.sb([128, 2, 256])
    for k in range(3):
        P.dma(cwb[:, k, :], cw[k:k + 1, :].partition_broadcast(128), writes=["cwb"])
    P.dma(cbb[:], cb.partition_broadcast(128), writes=["cbb"])
    for n in range(2):
        P.dma(skb[:, n, :], skip[n:n + 1, :].partition_broadcast(128), writes=["skb"])
    w1s = P.sb([33, 64]); w2s = P.sb([64, 64]); w3s = P.sb([64, 64]); fbs = P.sb([64, 4]); wos = P.sb([64, 1024]); m0s = P.sb([128, 1])
    P.dma(w1s[:], fw1, writes=["w1s"]); P.dma(w2s[:], fw2, writes=["w2s"]); P.dma(w3s[:], fw3, writes=["w3s"])
    P.dma(fbs[:], fb, writes=["fbs"]); P.dma(wos[:], fwo, writes=["wos"]); P.dma(m0s[:], m0, writes=["m0s"])
    bfr = P.sb([64, 3])
    P.op("dve", lambda e: e.tensor_scalar(out=bfr[:], in0=fbs[:, 0:3], scalar1=fbs[:, 3:4], scalar2=None, op0=ALU.mult),
         reads=["fbs"], writes=["bfr"])
    ones = P.sb([128, 128])
    P.op("pool", lambda e: e.memset(ones[:], 1.0), writes=["ones"])
    zb = P.sb([128, LTM, 256], BF16)
    Sb = P.sb([128, FM, 256], BF16); Db = P.sb([128, FM, 256], BF16)
    Hre = P.sb([128, FM, 256], BF16); Him = P.sb([128, FM, 256], BF16)
    Yre, Yim = Sb, Db
    strip = Rot(P, 4, [128, FM * 128], BF16, tag="strip")
    h3T = P.sb([64, HY_L])
    rn = P.sb([128, 256])
    psM = Rot(P, 2, [128, 512], F32, psum=True, tag="psM")
    psF = Rot(P, 2, [128, 512], F32, psum=True, tag="psF")
    psN = P.ps([128, 512])
    psD = Rot(P, 2, [128, 512], F32, psum=True, tag="psD")
    t512 = Rot(P, 5, [128, 512], F32, tag="t512")
    ti = Rot(P, 2, [64, 512], I32, tag="ti")
    t768 = Rot(P, 2, [128, 768], F32, tag="t768")
    t256 = Rot(P, 6, [128, 256], F32, tag="t256")
    TWO_PI = float(2 * np.pi)

    for s, L in seqs:
        d = din[s]
        LT, F = L // 128, (L + 1 + 127) // 128
        for tt in range(LT):
            a0, a0k = t768.next(); a1, a1k = t768.next()
            P.dma(a0[:], d["phy"][tt * 128: tt * 128 + 128, :], writes=[a0k])
            P.dma(a1[:], d["phy"][tt * 128 + 1: tt * 128 + 129, :], writes=[a1k])
            P.op("dve", lambda e, a0=a0: e.tensor_tensor(out=a0[:], in0=a0[:], in1=cwb[:, 0, :], op=ALU.mult), reads=[a0k, "cwb"], writes=[a0k])
            P.op("pool", lambda e, a1=a1: e.tensor_tensor(out=a1[:], in0=a1[:], in1=cwb[:, 1, :], op=ALU.mult), reads=[a1k, "cwb"], writes=[a1k])
            P.op("dve", lambda e, a0=a0, a1=a1: e.tensor_tensor(out=a0[:], in0=a0[:], in1=a1[:], op=ALU.add), reads=[a0k, a1k], writes=[a0k])
            P.dma(a1[:], d["phy"][tt * 128 + 2: tt * 128 + 130, :], writes=[a1k])
            P.op("pool", lambda e, a1=a1: e.tensor_tensor(out=a1[:], in0=a1[:], in1=cwb[:, 2, :], op=ALU.mult), reads=[a1k, "cwb"], writes=[a1k])
            P.op("dve", lambda e, a0=a0, a1=a1: e.tensor_tensor(out=a0[:], in0=a0[:], in1=a1[:], op=ALU.add), reads=[a0k, a1k], writes=[a0k])
            P.op("dve", lambda e, a0=a0: e.tensor_tensor(out=a0[:], in0=a0[:], in1=cbb[:], op=ALU.add), reads=[a0k, "cbb"], writes=[a0k])
            P.op("act", lambda e, a0=a0, tt=tt: e.copy(out=zb[:, tt, :], in_=a0[:, 0:256]), reads=[a0k], writes=[("z", tt)])
            P.dma(d["u"][tt * 128:(tt + 1) * 128, :], a0[:], reads=[a0k], writes=[("u", s, tt)])
        for (c0, n) in chunks_of(L):
            featsb, fk_ = t512.next()
            P.dma(featsb[0:33, :n], d["feats"][:, c0:c0 + n], writes=[fk_])
            src, srck, wl, wk_ = featsb, fk_, w1s, "w1s"
            for layer in range(3):
                kdim = 33 if layer == 0 else 64
                pt, pk_ = psM.next()
                P.op("pe", lambda e, pt=pt, wl=wl, src=src, kdim=kdim, c0=c0, n=n, layer=layer: e.matmul(
                    pt[0:64, :n], wl[:], src[0:kdim, :n], start=True, stop=True),
                    reads=[srck, wk_], writes=[pk_])
                arg, argk = t512.next()
                P.op("dve", lambda e, arg=arg, pt=pt, n=n, layer=layer: e.tensor_scalar(
                    out=arg[0:64, :n], in0=pt[0:64, :n], scalar1=fbs[:, 3:4], scalar2=bfr[:, layer:layer + 1], op0=ALU.mult, op1=ALU.add),
                    reads=[pk_, "fbs", "bfr"], writes=[argk])
                it, itk = ti.next()
                P.op("dve", lambda e, it=it, arg=arg, n=n: e.tensor_scalar(
                    out=it[:, :n], in0=arg[0:64, :n], scalar1=1.0 / TWO_PI, scalar2=None, op0=ALU.mult), reads=[argk], writes=[itk])
                tf, tfk = t512.next()
                P.op("dve", lambda e, tf=tf, it=it, n=n: e.tensor_copy(out=tf[0:64, :n], in_=it[:, :n]), reads=[itk], writes=[tfk])
                P.op("dve", lambda e, tf=tf, arg=arg, n=n: e.scalar_tensor_tensor(
                    out=arg[0:64, :n], in0=tf[0:64, :n], scalar=-TWO_PI, in1=arg[0:64, :n], op0=ALU.mult, op1=ALU.add),
                    reads=[tfk, argk], writes=[argk])
                if layer < 2:
                    hn, hnk = t512.next()
                    P.op("act", lambda e, hn=hn, arg=arg, n=n: e.activation(out=hn[0:64, :n], in_=arg[0:64, :n], func=AF.Sin),
                         reads=[argk], writes=[hnk])
                    src, srck = hn, hnk
                    wl, wk_ = (w2s, "w2s") if layer == 0 else (w3s, "w3s")
                else:
                    P.op("act", lambda e, arg=arg, n=n, c0=c0: e.activation(out=h3T[:, c0:c0 + n], in_=arg[0:64, :n], func=AF.Sin),
                         reads=[argk], writes=[("h3T", c0)])
        for order in range(2):
            for lt in range(LT):
                dec, deck = t256.next()
                P.dma(dec[:], d["decay"][lt * 128:(lt + 1) * 128, :], writes=[deck])
                pf, pfk = psF.next()
                for dr in range(2):
                    col = dr * 512 + order * 256
                    P.op("pe", lambda e, pf=pf, lt=lt, dr=dr, col=col: e.matmul(
                        pf[:, dr * 256:(dr + 1) * 256], h3T[:, lt * 128:(lt + 1) * 128], wos[:, col:col + 256], start=True, stop=True),
                        reads=[("h3T", (lt * 128) // 512 * 512), "wos"], writes=[pfk], accum=dr > 0)
                fb_, fbk = t512.next()
                P.op("dve", lambda e, fb_=fb_, pf=pf, dec=dec: e.tensor_tensor(
                    out=fb_[:].rearrange("p (a b) -> p a b", a=2), in0=pf[:].rearrange("p (a b) -> p a b", a=2),
                    in1=dec[:].unsqueeze(1).to_broadcast([128, 2, 256]), op=ALU.mult), reads=[pfk, deck], writes=[fbk])
                if lt == 0:
                    P.op("dve", lambda e, fb_=fb_: e.tensor_scalar(out=fb_[:, 256:512], in0=fb_[:, 256:512], scalar1=m0s[:, 0:1], scalar2=None, op0=ALU.mult),
                         reads=[fbk, "m0s"], writes=[fbk])
                P.op("pool", lambda e, fb_=fb_, lt=lt: e.tensor_tensor(out=Sb[:, lt, :], in0=fb_[:, 0:256], in1=fb_[:, 256:512], op=ALU.add),
                     reads=[fbk], writes=[("S", lt)])
                P.op("pool", lambda e, fb_=fb_, lt=lt: e.tensor_tensor(out=Db[:, lt, :], in0=fb_[:, 0:256], in1=fb_[:, 256:512], op=ALU.subtract),
                     reads=[fbk], writes=[("D", lt)])
                sq, sqk = t512.next()
                P.op("act", lambda e, sq=sq, fb_=fb_: e.activation(out=sq[:], in_=fb_[:], func=AF.Square), reads=[fbk], writes=[sqk])
                for hh in range(2):
                    P.op("pe", lambda e, sq=sq, hh=hh, lt=lt, LT=LT: e.matmul(
                        psN[:, 0:256], ones[:], sq[:, hh * 256:(hh + 1) * 256], start=(lt == 0 and hh == 0), stop=(lt == LT - 1 and hh == 1)),
                        reads=[sqk, "ones"], writes=["psN"], accum=not (lt == 0 and hh == 0))
            P.op("act", lambda e: e.activation(out=rn[:], in_=psN[:, 0:256], func=AF.Sqrt, bias=EPS, scale=1.0), reads=["psN"], writes=["rn"])
            P.op("dve", lambda e: e.reciprocal(out=rn[:], in_=rn[:]), reads=["rn"], writes=["rn"])
            for fb_i in range(F):
                sc_, sck = strip.next(); ss_, ssk = strip.next()
                P.dma(sc_[:, :LT * 128], d["Wc"][fb_i].rearrange("p a b -> p (a b)"), writes=[sck])
                P.dma(ss_[:, :LT * 128], d["Ws"][fb_i].rearrange("p a b -> p (a b)"), writes=[ssk])
                pd, pdk = psD.next()
                for lt in range(LT):
                    P.op("pe", lambda e, pd=pd, sc_=sc_, lt=lt, LT=LT: e.matmul(pd[:, 0:256], sc_[:, lt * 128:(lt + 1) * 128], Sb[:, lt, :], start=(lt == 0), stop=(lt == LT - 1)),
                         reads=[sck, ("S", lt)], writes=[pdk], accum=lt > 0)
                for lt in range(LT):
                    P.op("pe", lambda e, pd=pd, ss_=ss_, lt=lt, LT=LT: e.matmul(pd[:, 256:512], ss_[:, lt * 128:(lt + 1) * 128], Db[:, lt, :], start=(lt == 0), stop=(lt == LT - 1)),
                         reads=[ssk, ("D", lt)], writes=[pdk], accum=True)
                P.op("dve", lambda e, pd=pd, fb_i=fb_i: e.tensor_tensor(out=Hre[:, fb_i, :], in0=pd[:, 0:256], in1=rn[:], op=ALU.mult), reads=[pdk, "rn"], writes=[("Hre", fb_i)])
                P.op("dve", lambda e, pd=pd, fb_i=fb_i: e.tensor_tensor(out=Him[:, fb_i, :], in0=pd[:, 256:512], in1=rn[:], op=ALU.mult), reads=[pdk, "rn"], writes=[("Him", fb_i)])
            for fb_i in range(F):
                sc_, sck = strip.next(); ss_, ssk = strip.next()
                P.dma(sc_[:, :LT * 128], d["Wc"][fb_i].rearrange("p a b -> p (a b)"), writes=[sck])
                P.dma(ss_[:, :LT * 128], d["Ws"][fb_i].rearrange("p a b -> p (a b)"), writes=[ssk])
                pd, pdk = psD.next()
                for lt in range(LT):
                    P.op("pe", lambda e, pd=pd, sc_=sc_, lt=lt, LT=LT: e.matmul(pd[:, 0:256], sc_[:, lt * 128:(lt + 1) * 128], zb[:, lt, :], start=(lt == 0), stop=(lt == LT - 1)),
                         reads=[sck, ("z", lt)], writes=[pdk], accum=lt > 0)
                for lt in range(LT):
                    P.op("pe", lambda e, pd=pd, ss_=ss_, lt=lt, LT=LT: e.matmul(pd[:, 256:512], ss_[:, lt * 128:(lt + 1) * 128], zb[:, lt, :], start=(lt == 0), stop=(lt == LT - 1)),
                         reads=[ssk, ("z", lt)], writes=[pdk], accum=True)
                a, ak = t256.next(); b, bk = t256.next(); c, ck = t256.next(); dd, dk = t256.next()
                P.op("dve", lambda e, a=a, pd=pd, fb_i=fb_i: e.tensor_tensor(out=a[:], in0=pd[:, 0:256], in1=Hre[:, fb_i, :], op=ALU.mult), reads=[pdk, ("Hre", fb_i)], writes=[ak])
                P.op("dve", lambda e, b=b, pd=pd, fb_i=fb_i: e.tensor_tensor(out=b[:], in0=pd[:, 256:512], in1=Him[:, fb_i, :], op=ALU.mult), reads=[pdk, ("Him", fb_i)], writes=[bk])
                P.op("dve", lambda e, c=c, pd=pd, fb_i=fb_i: e.tensor_tensor(out=c[:], in0=pd[:, 0:256], in1=Him[:, fb_i, :], op=ALU.mult), reads=[pdk, ("Him", fb_i)], writes=[ck])
                P.op("dve", lambda e, dd=dd, pd=pd, fb_i=fb_i: e.tensor_tensor(out=dd[:], in0=pd[:, 256:512], in1=Hre[:, fb_i, :], op=ALU.mult), reads=[pdk, ("Hre", fb_i)], writes=[dk])
                P.op("pool", lambda e, a=a, b=b, fb_i=fb_i: e.tensor_tensor(out=Yre[:, fb_i, :], in0=a[:], in1=b[:], op=ALU.subtract), reads=[ak, bk], writes=[("S", fb_i)])
                P.op("pool", lambda e, c=c, dd=dd, fb_i=fb_i: e.tensor_tensor(out=Yim[:, fb_i, :], in0=c[:], in1=dd[:], op=ALU.add), reads=[ck, dk], writes=[("D", fb_i)])
            for tt in range(LT):
                sc_, sck = strip.next(); ss_, ssk = strip.next()
                P.dma(sc_[:, :F * 128], d["Vc"][tt].rearrange("p a b -> p (a b)"), writes=[sck])
                P.dma(ss_[:, :F * 128], d["Vs"][tt].rearrange("p a b -> p (a b)"), writes=[ssk])
                zt, ztk = t256.next(); xt, xtk = t256.next()
                if order == 0:
                    P.dma(zt[:], d["u"][tt * 128:(tt + 1) * 128, 0:256], reads=[("u", s, tt)], writes=[ztk])
                else:
                    P.dma(zt[:], d["z1"][tt * 128:(tt + 1) * 128, :], reads=[("z1", s, tt)], writes=[ztk])
                P.dma(xt[:], d["u"][tt * 128:(tt + 1) * 128, 256 * (order + 1):256 * (order + 2)], reads=[("u", s, tt)], writes=[xtk])
                pd, pdk = psD.next()
                for ft in range(F):
                    P.op("pe", lambda e, pd=pd, sc_=sc_, ft=ft: e.matmul(pd[:, 0:256], sc_[:, ft * 128:(ft + 1) * 128], Yre[:, ft, :], start=(ft == 0), stop=False),
                         reads=[sck, ("S", ft)], writes=[pdk], accum=ft > 0)
                for ft in range(F):
                    P.op("pe", lambda e, pd=pd, ss_=ss_, ft=ft, F=F: e.matmul(pd[:, 0:256], ss_[:, ft * 128:(ft + 1) * 128], Yim[:, ft, :], start=False, stop=(ft == F - 1)),
                         reads=[ssk, ("D", ft)], writes=[pdk], accum=True)
                P.op("pool", lambda e, zt=zt, order=order: e.tensor_tensor(out=zt[:], in0=zt[:], in1=skb[:, order, :], op=ALU.mult), reads=[ztk, "skb"], writes=[ztk])
                P.op("dve", lambda e, zt=zt, pd=pd: e.tensor_tensor(out=zt[:], in0=pd[:, 0:256], in1=zt[:], op=ALU.add), reads=[pdk, ztk], writes=[ztk])
                P.op("dve", lambda e, zt=zt, xt=xt: e.tensor_tensor(out=zt[:], in0=zt[:], in1=xt[:], op=ALU.mult), reads=[ztk, xtk], writes=[ztk])
                if order == 0:
                    P.op("act", lambda e, zt=zt, tt=tt: e.copy(out=zb[:, tt, :], in_=zt[:]), reads=[ztk], writes=[("z", tt)])
                    P.dma(d["z1"][tt * 128:(tt + 1) * 128, :], zt[:], reads=[ztk], writes=[("z1", s, tt)])
                else:
                    P.dma(d["out"][tt * 128:(tt + 1) * 128, :], zt[:], reads=[ztk], writes=[("out", s, tt)])
    return P.finish()


def rope_tables():
    t = np.arange(4096)
    row = (t // 64).astype(np.float32)
    col = (t % 64).astype(np.float32)
    inv = (np.float32(10000.0) ** (-np.arange(16, dtype=np.float32) / np.float32(16))).astype(np.float32)
    ar = (row[:, None] * inv[None, :]).astype(np.float32)
    ac = (col[:, None] * inv[None, :]).astype(np.float32)
    C = np.concatenate([np.cos(ar), np.cos(ar), np.cos(ac), np.cos(ac)], 1).astype(np.float32)
    S = np.concatenate([-np.sin(ar), np.sin(ar), -np.sin(ac), np.sin(ac)], 1).astype(np.float32)
    return C, S


def attn_masks():
    j = np.arange(128)[:, None]
    r = np.arange(128)[None, :]
    mL = (j >= r).astype(np.float32)
    mR = (j <= r).astype(np.float32)
    return np.tile(mL, (1, 4)), np.tile(mR, (1, 4))


def build_att():
    P = Prog()
    L, C = 4096, 256
    LT, CT = L // 128, C // 128
    q = P.dram_in("q", [L, 256]); k = P.dram_in("k", [L, 64]); v = P.dram_in("v", [L, 64])
    qc = P.dram_in("qc", [C, 256]); kc = P.dram_in("kc", [C, 64]); vc = P.dram_in("vc", [C, 64])
    qg = P.dram_in("qg", [1, 64]); kg = P.dram_in("kg", [1, 64]); sink = P.dram_in("sink", [1, 4])
    ropeC = P.dram_in("ropeC", [L, 64]); ropeS = P.dram_in("ropeS", [L, 64])
    maskL = P.dram_in("maskL", [128, 512]); maskR = P.dram_in("maskR", [128, 512]); ident = P.dram_in("ident", [128, 128])
    att = P.dram_out("att", [L, 256]); attc = P.dram_out("attc", [C, 256])
    idn = P.sb([128, 128]); mL = P.sb([128, 512]); mR = P.sb([128, 512])
    P.dma(idn[:], ident, writes=["idn"]); P.dma(mL[:], maskL, writes=["mL"]); P.dma(mR[:], maskR, writes=["mR"])
    g = P.sb([128, 2, 64]); gsw = P.sb([128, 2, 64]); esk = P.sb([128, 4])
    P.dma(g[:, 0, :], qg.partition_broadcast(128), writes=["g"]); P.dma(g[:, 1, :], kg.partition_broadcast(128), writes=["g"])
    P.dma(esk[:], sink.partition_broadcast(128), writes=["esk"])
    P.op("act", lambda e: e.activation(out=esk[:], in_=esk[:], func=AF.Exp), reads=["esk"], writes=["esk"])
    P.op("dve", lambda e: e.tensor_scalar(out=g[:, 0, :], in0=g[:, 0, :], scalar1=0.125, scalar2=None, op0=ALU.mult), reads=["g"], writes=["g"])
    g5 = g[:].rearrange("p a (r h i) -> p a r h i", r=2, h=2)
    s5 = gsw[:].rearrange("p a (r h i) -> p a r h i", r=2, h=2)
    for a in range(2):
        P.op("dve", lambda e, a=a: e.tensor_copy(out=s5[:, a, :, 0, :], in_=g5[:, a, :, 1, :]), reads=["g"], writes=["gsw"])
        P.op("dve", lambda e, a=a: e.tensor_copy(out=s5[:, a, :, 1, :], in_=g5[:, a, :, 0, :]), reads=["g"], writes=["gsw"])
    qT = P.sb([64, 4, L], BF16); qcT = P.sb([64, 4, C], BF16); kT = P.sb([64, L + C], BF16)
    Vx = P.sb([128, LT + CT, 65], BF16)
    P.op("pool", lambda e: e.memset(Vx[:, :, 64:65], 1.0), writes=["Vx1"])
    tq = Rot(P, 3, [128, 256], F32, tag="tq"); t64 = Rot(P, 4, [128, 64], F32, tag="t64")
    tab = Rot(P, 8, [128, 64], F32, tag="tab"); t2r = Rot(P, 4, [128, 64], F32, tag="t2r")
    tr = Rot(P, 3, [128, 256], F32, tag="tr"); t4 = Rot(P, 4, [128, 4], F32, tag="t4")
    psT = Rot(P, 2, [64, 512], F32, psum=True, tag="psT")
    psS = Rot(P, 3, [128, 512], F32, psum=True, tag="psS")
    psO = Rot(P, 2, [128, 512], F32, psum=True, tag="psO")
    pTb = Rot(P, 7, [128, 512], BF16, tag="pT"); pE = Rot(P, 2, [128, 512], F32, tag="pE")

    def normrope(x, xk, nh, a, tabs, dst_fn, dkeys):
        sq, sqk = tq.next()
        P.op("act", lambda e: e.activation(out=sq[:, :nh * 64], in_=x[:, :nh * 64], func=AF.Square), reads=[xk], writes=[sqk])
        ss, ssk = t4.next()
        P.op("dve", lambda e: e.tensor_reduce(out=ss[:, :nh], in_=sq[:, :nh * 64].rearrange("p (h d) -> p h d", d=64), axis=AX.X, op=ALU.add),
             reads=[sqk], writes=[ssk])
        P.op("act", lambda e: e.activation(out=ss[:, :nh], in_=ss[:, :nh], func=AF.Sqrt, bias=EPS, scale=1.0 / 64), reads=[ssk], writes=[ssk])
        P.op("dve", lambda e: e.reciprocal(out=ss[:, :nh], in_=ss[:, :nh]), reads=[ssk], writes=[ssk])
        xr, xrk = tr.next()
        for h in range(nh):
            xh = x[:, h * 64:(h + 1) * 64]
            oh = xr[:, h * 64:(h + 1) * 64]
            if tabs is None:
                P.op("dve", lambda e, xh=xh, oh=oh, h=h: e.scalar_tensor_tensor(out=oh, in0=xh, scalar=ss[:, h:h + 1], in1=g[:, a, :], op0=ALU.mult, op1=ALU.mult),
                     reads=[xk, ssk, "g"], writes=[xrk])
            else:
                Cx, Cxk, Sx, Sxk = tabs
                t2, t2k = t2r.next()
                xh4 = xh.rearrange("p (r h i) -> p r h i", r=2, h=2)
                t24 = t2[:].rearrange("p (r h i) -> p r h i", r=2, h=2)
                S4 = Sx[:].rearrange("p (r h i) -> p r h i", r=2, h=2)
                P.op("dve", lambda e, xh=xh, oh=oh, h=h: e.scalar_tensor_tensor(out=oh, in0=xh, scalar=ss[:, h:h + 1], in1=Cx[:], op0=ALU.mult, op1=ALU.mult),
                     reads=[xk, ssk, Cxk], writes=[xrk])
                P.op("dve", lambda e, xh4=xh4, t24=t24, S4=S4, h=h: e.scalar_tensor_tensor(out=t24[:, :, 0, :], in0=xh4[:, :, 1, :], scalar=ss[:, h:h + 1], in1=S4[:, :, 0, :], op0=ALU.mult, op1=ALU.mult),
                     reads=[xk, ssk, Sxk], writes=[t2k])
                P.op("dve", lambda e, xh4=xh4, t24=t24, S4=S4, h=h: e.scalar_tensor_tensor(out=t24[:, :, 1, :], in0=xh4[:, :, 0, :], scalar=ss[:, h:h + 1], in1=S4[:, :, 1, :], op0=ALU.mult, op1=ALU.mult),
                     reads=[xk, ssk, Sxk], writes=[t2k])
                P.op("pool", lambda e, oh=oh, t2=t2: e.tensor_tensor(out=oh, in0=oh, in1=t2[:], op=ALU.add), reads=[xrk, t2k], writes=[xrk])
        pt, ptk = psT.next()
        for h in range(nh):
            P.op("pe", lambda e, pt=pt, h=h: e.transpose(out=pt[:, h * 128:(h + 1) * 128], in_=xr[:, h * 64:(h + 1) * 64], identity=idn[:]),
                 reads=[xrk, "idn"], writes=[ptk], accum=h > 0)
        for h in range(nh):
            P.op("act", lambda e, pt=pt, h=h: e.copy(out=dst_fn(h), in_=pt[:, h * 128:(h + 1) * 128]), reads=[ptk], writes=dkeys)

    for tt in range(LT + CT):
        isc = tt >= LT
        r0 = (tt - LT) * 128 if isc else tt * 128
        qsrc, ksrc, vsrc = (qc, kc, vc) if isc else (q, k, v)
        xq, xqk = tq.next(); xk_, xkk = t64.next(); xv, xvk = t64.next()
        P.dma(xq[:], qsrc[r0:r0 + 128, :], writes=[xqk]); P.dma(xk_[:], ksrc[r0:r0 + 128, :], writes=[xkk]); P.dma(xv[:], vsrc[r0:r0 + 128, :], writes=[xvk])
        P.op("pool", lambda e, xv=xv, tt=tt: e.tensor_copy(out=Vx[:, tt, 0:64], in_=xv[:]), reads=[xvk], writes=[("Vx", tt)])
        if isc:
            tabq = tabk = None
        else:
            Cq, Cqk = tab.next(); Sq, Sqk = tab.next(); Ck, Ckk = tab.next(); Sk, Skk = tab.next()
            P.dma(Cq[:], ropeC[r0:r0 + 128, :], writes=[Cqk]); P.dma(Sq[:], ropeS[r0:r0 + 128, :], writes=[Sqk])
            P.op("pool", lambda e, Ck=Ck, Cq=Cq: e.tensor_tensor(out=Ck[:], in0=Cq[:], in1=g[:, 1, :], op=ALU.mult), reads=[Cqk, "g"], writes=[Ckk])
            P.op("pool", lambda e, Sk=Sk, Sq=Sq: e.tensor_tensor(out=Sk[:], in0=Sq[:], in1=gsw[:, 1, :], op=ALU.mult), reads=[Sqk, "gsw"], writes=[Skk])
            P.op("pool", lambda e, Cq=Cq: e.tensor_tensor(out=Cq[:], in0=Cq[:], in1=g[:, 0, :], op=ALU.mult), reads=[Cqk, "g", Ckk], writes=[Cqk])
            P.op("pool", lambda e, Sq=Sq: e.tensor_tensor(out=Sq[:], in0=Sq[:], in1=gsw[:, 0, :], op=ALU.mult), reads=[Sqk, "gsw", Skk], writes=[Sqk])
            tabq = (Cq, Cqk, Sq, Sqk); tabk = (Ck, Ckk, Sk, Skk)
        if isc:
            normrope(xq, xqk, 4, 0, tabq, lambda h, r0=r0: qcT[:, h, r0:r0 + 128], [("qcT", tt)])
            normrope(xk_, xkk, 1, 1, tabk, lambda h, r0=r0: kT[:, L + r0:L + r0 + 128], [("kT", tt)])
        else:
            normrope(xq, xqk, 4, 0, tabq, lambda h, r0=r0: qT[:, h, r0:r0 + 128], [("qT", tt)])
            normrope(xk_, xkk, 1, 1, tabk, lambda h, r0=r0: kT[:, r0:r0 + 128], [("kT", tt)])

    def attend(qsrcT, qkey, i, kblocks, out_d):
        pts = []
        for (kb, msk) in kblocks:
            ps, psk = psS.next()
            P.op("pe", lambda e, ps=ps, kb=kb: e.matmul(ps[:], kT[:, kb * 128:(kb + 1) * 128], qsrcT[:, :, i * 128:(i + 1) * 128], start=True, stop=True),
                 reads=[("kT", kb), qkey], writes=[psk])
            pt, ptk = pTb.next()
            if msk is None:
                P.op("act", lambda e, pt=pt, ps=ps: e.activation(out=pt[:], in_=ps[:], func=AF.Exp), reads=[psk], writes=[ptk])
            else:
                pe_, pek = pE.next()
                P.op("act", lambda e, pe_=pe_, ps=ps: e.activation(out=pe_[:], in_=ps[:], func=AF.Exp), reads=[psk], writes=[pek])
                mt, mk = msk
                P.op("dve", lambda e, pt=pt, pe_=pe_, mt=mt: e.tensor_tensor(out=pt[:], in0=pe_[:], in1=mt[:], op=ALU.mult), reads=[pek, mk], writes=[ptk])
            pts.append((pt, ptk, kb))
        po, pok = psO.next()
        first = True
        for h in range(4):
            for n_, (pt, ptk, kb) in enumerate(pts):
                P.op("pe", lambda e, po=po, pt=pt, kb=kb, h=h, n_=n_: e.matmul(po[:, h * 128:h * 128 + 65], pt[:, h * 128:(h + 1) * 128], Vx[:, kb, :],
                                                                    start=(n_ == 0), stop=(n_ == len(pts) - 1)),
                     reads=[ptk, ("Vx", kb), "Vx1"], writes=[pok], accum=not first)
                first = False
        den, denk = t4.next()
        P.op("dve", lambda e, den=den, po=po: e.tensor_tensor(out=den[:], in0=po[:].rearrange("p (h c) -> p h c", c=128)[:, :, 64], in1=esk[:], op=ALU.add),
             reads=[pok, "esk"], writes=[denk])
        P.op("dve", lambda e, den=den: e.reciprocal(out=den[:], in_=den[:]), reads=[denk], writes=[denk])
        ot, otk = tq.next()
        for h in range(4):
            P.op("act", lambda e, ot=ot, po=po, den=den, h=h: e.activation(out=ot[:, h * 64:(h + 1) * 64], in_=po[:, h * 128:h * 128 + 64], func=AF.Copy, scale=den[:, h:h + 1]),
                 reads=[pok, denk], writes=[otk])
        P.dma(out_d[i * 128:(i + 1) * 128, :], ot[:], reads=[otk], writes=[("o", id(out_d), i)])

    for i in range(LT):
        kbs = []
        if i >= 1:
            kbs.append((i - 1, (mL, "mL")))
        kbs.append((i, None))
        if i <= LT - 2:
            kbs.append((i + 1, (mR, "mR")))
        kbs += [(LT, None), (LT + 1, None)]
        attend(qT, ("qT", i), i, kbs, att)
    for i in range(CT):
        attend(qcT, ("qcT", LT + i), i, [(LT, None), (LT + 1, None)], attc)
    return P.finish()


SIG_CLAMP = float(1.0 / (1.0 + np.exp(-1.702 * 7.0)))
NE = 32


def split_chunks(lo, hi, bound, n=512):
    out = []
    c = lo
    while c < hi:
        e = min(c + n, hi)
        if c < bound < e:
            e = bound
        out.append((c, e - c))
        c = e
    return out


def build_p3a(T, ctx0, odd, ngroups=2):
    P = Prog()
    xT = P.dram_in("xT", [D, T]); yT = P.dram_in("yT", [D, T]); cT = P.dram_in("cT", [D, 2])
    modw = P.dram_in("modw", [D, 4096]); modb = P.dram_in("modb", [4096]); g2n = P.dram_in("g", [D])
    wo = P.dram_in("wo", [D, D])
    rw = P.dram_in("rw", [D, NE]); rb = P.dram_in("rb", [1, NE])
    ident = P.dram_in("ident", [128, 128])
    if odd:
        lng = P.dram_in("lng", [D]); lnb = P.dram_in("lnb", [D]); b2 = P.dram_in("b2", [D])
    x1o = P.dram_out("x1", [D, T]); hTo = P.dram_out("hT", [D, T], BF16); gwo = P.dram_out("gw", [NE, T])
    TG = (T + ngroups - 1) // ngroups
    psA = Rot(P, 2, [128, 512], F32, psum=True, tag="psA")
    psG = Rot(P, 2, [128, 512], F32, psum=True, tag="psG")
    psU = Rot(P, 2, [128, 512], F32, psum=True, tag="psU")
    psDn = Rot(P, 2, [128, 512], F32, psum=True, tag="psDn")
    R16 = Rot(P, 2, [128, KB, 512], F32, tag="R16")
    mv, mvk = adaln_vecs(P, cT, modw, modb, 4, psA, "ad", wbuf=R16)
    idn = P.sb([128, 128]); P.dma(idn[:], ident, writes=["idn"])
    gs = P.sb([128, KB]); P.dma(gs[:], g2n.rearrange("(kb p) -> p kb", p=128), writes=["gs"], allow_slow_non_contiguous=True)
    A = P.sb([128, KB, 2])
    P.op("dve", lambda e: e.tensor_scalar(out=A[:], in0=mv[:, 16:24, :], scalar1=1.0, scalar2=None, op0=ALU.add), reads=[mvk], writes=["A"])
    P.op("dve", lambda e: e.tensor_tensor(out=A[:], in0=A[:], in1=gs[:].unsqueeze(2).to_broadcast([128, KB, 2]), op=ALU.mult), reads=["A", "gs"], writes=["A"])
    ones = P.sb([128, 128]); P.op("pool", lambda e: e.memset(ones[:], 1.0 / D), writes=["ones"])
    if odd:
        lgs = P.sb([128, KB]); lbs = P.sb([128, KB]); b2s = P.sb([128, KB])
        P.dma(lgs[:], lng.rearrange("(kb p) -> p kb", p=128), writes=["lgs"], allow_slow_non_contiguous=True)
        P.dma(lbs[:], lnb.rearrange("(kb p) -> p kb", p=128), writes=["lbs"], allow_slow_non_contiguous=True)
        P.dma(b2s[:], b2.rearrange("(kb p) -> p kb", p=128), writes=["b2s"], allow_slow_non_contiguous=True)
    rws = P.sb([128, KB, NE]); rbb = P.sb([128, NE])
    P.dma(rws[:], rw.rearrange("(kb p) n -> p kb n", p=128), writes=["rws"])
    P.dma(rbb[:], rb.partition_broadcast(128), writes=["rbb"])
    wos = P.sb([128, KB, D], BF16)
    for kb in range(KB):
        P.dma(wos[:, kb, :], wo[kb * 128:(kb + 1) * 128, :], writes=[("wos", kb)], queue="pool")
    xres = P.sb([128, KB, TG]); hT = P.sb([128, KB, TG], BF16)
    gwT = P.sb([32, TG])
    R8 = Rot(P, 2, [128, KB, 512], BF16, tag="R8")
    t512 = Rot(P, 6, [128, 512], F32, tag="t512")
    t32 = Rot(P, 6, [128, NE], F32, tag="t32"); t8 = Rot(P, 3, [128, 8], F32, tag="t8")

    for gi in range(ngroups):
        lo, hi = gi * TG, min(T, (gi + 1) * TG)
        chs = split_chunks(lo, hi, ctx0)
        P.dma(xres[:, :, :hi - lo], xT[:, lo:hi].rearrange("(kb p) t -> p kb t", p=128), writes=[("xres", c0) for c0, _ in chs])
        for (c0, n) in chs:
            kind = 1 if c0 >= ctx0 else 0
            l0 = c0 - lo
            yt, ytk = R16.next()
            P.dma(yt[:, :, :n], yT[:, c0:c0 + n].rearrange("(kb p) t -> p kb t", p=128), writes=[ytk])
            yb, ybk = R8.next()
            if not odd:
                P.op("act", lambda e, yb=yb, yt=yt, n=n: e.copy(out=yb[:, :, :n], in_=yt[:, :, :n]), reads=[ytk], writes=[ybk])
            else:
                sq, sqk = R16.next()
                P.op("act", lambda e, sq=sq, yt=yt, n=n: e.activation(out=sq[:, :, :n], in_=yt[:, :, :n], func=AF.Square), reads=[ytk], writes=[sqk])
                pm, pmk = psA.next(); pq, pqk = psA.next()
                for kb in range(KB):
                    P.op("pe", lambda e, pm=pm, yt=yt, kb=kb, n=n: e.matmul(pm[:, :n], ones[:], yt[:, kb, :n], start=(kb == 0), stop=(kb == KB - 1)),
                         reads=[ytk, "ones"], writes=[pmk], accum=kb > 0)
                for kb in range(KB):
                    P.op("pe", lambda e, pq=pq, sq=sq, kb=kb, n=n: e.matmul(pq[:, :n], ones[:], sq[:, kb, :n], start=(kb == 0), stop=(kb == KB - 1)),
                         reads=[sqk, "ones"], writes=[pqk], accum=kb > 0)
                mean, meank = t512.next(); var, vark = t512.next()
                P.op("act", lambda e, mean=mean, pm=pm, n=n: e.copy(out=mean[:, :n], in_=pm[:, :n]), reads=[pmk], writes=[meank])
                P.op("dve", lambda e, var=var, mean=mean, n=n: e.tensor_tensor(out=var[:, :n], in0=mean[:, :n], in1=mean[:, :n], op=ALU.mult), reads=[meank], writes=[vark])
                P.op("dve", lambda e, var=var, pq=pq, n=n: e.tensor_tensor(out=var[:, :n], in0=pq[:, :n], in1=var[:, :n], op=ALU.subtract), reads=[pqk, vark], writes=[vark])
                P.op("act", lambda e, var=var, n=n: e.activation(out=var[:, :n], in_=var[:, :n], func=AF.Sqrt, bias=EPS, scale=1.0), reads=[vark], writes=[vark])
                P.op("dve", lambda e, var=var, n=n: e.reciprocal(out=var[:, :n], in_=var[:, :n]), reads=[vark], writes=[vark])
                for kb in range(KB):
                    P.op("dve", lambda e, yt=yt, mean=mean, kb=kb, n=n: e.tensor_tensor(out=yt[:, kb, :n], in0=yt[:, kb, :n], in1=mean[:, :n], op=ALU.subtract), reads=[ytk, meank], writes=[ytk])
                    P.op("pool", lambda e, yt=yt, var=var, kb=kb, n=n: e.tensor_tensor(out=yt[:, kb, :n], in0=yt[:, kb, :n], in1=var[:, :n], op=ALU.mult), reads=[ytk, vark], writes=[ytk])
                    P.op("act", lambda e, yt=yt, yb=yb, kb=kb, n=n: e.activation(out=yb[:, kb, :n], in_=yt[:, kb, :n], func=AF.Silu, scale=lgs[:, kb:kb + 1], bias=lbs[:, kb:kb + 1]),
                         reads=[ytk, "lgs", "lbs"], writes=[ybk])
            for db in range(KB):
                pt, pk = psDn.next()
                for kb in range(KB):
                    P.op("pe", lambda e, pt=pt, yb=yb, db=db, kb=kb, n=n: e.matmul(pt[:, :n], wos[:, kb, db * 128:(db + 1) * 128], yb[:, kb, :n], start=(kb == 0), stop=(kb == KB - 1)),
                         reads=[("wos", kb), ybk], writes=[pk], accum=kb > 0)
                if odd:
                    tb, tbk = t512.next()
                    P.op("act", lambda e, tb=tb, pt=pt, db=db, n=n: e.activation(out=tb[:, :n], in_=pt[:, :n], func=AF.Identity, bias=b2s[:, db:db + 1], scale=1.0), reads=[pk, "b2s"], writes=[tbk])
                    src, srck = tb, tbk
                else:
                    src, srck = pt, pk
                P.op("dve", lambda e, src=src, db=db, n=n, l0=l0, kind=kind: e.scalar_tensor_tensor(
                    out=xres[:, db, l0:l0 + n], in0=src[:, :n], scalar=mv[:, db, kind:kind + 1], in1=xres[:, db, l0:l0 + n], op0=ALU.mult, op1=ALU.add),
                    reads=[srck, mvk, ("xres", c0)], writes=[("xres", c0)])
        for (c0, n) in chs:
            kind = 1 if c0 >= ctx0 else 0
            l0 = c0 - lo
            sq, sqk = R16.next()
            P.op("act", lambda e, sq=sq, n=n, l0=l0: e.activation(out=sq[:, :, :n], in_=xres[:, :, l0:l0 + n], func=AF.Square), reads=[("xres", c0)], writes=[sqk])
            pt, pk = psA.next()
            for kb in range(KB):
                P.op("pe", lambda e, pt=pt, sq=sq, kb=kb, n=n: e.matmul(pt[:, :n], ones[:], sq[:, kb, :n], start=(kb == 0), stop=(kb == KB - 1)),
                     reads=[sqk, "ones"], writes=[pk], accum=kb > 0)
            rs, rsk = t512.next()
            P.op("act", lambda e, rs=rs, pt=pt, n=n: e.activation(out=rs[:, :n], in_=pt[:, :n], func=AF.Sqrt, bias=EPS, scale=1.0), reads=[pk], writes=[rsk])
            P.op("dve", lambda e, rs=rs, n=n: e.reciprocal(out=rs[:, :n], in_=rs[:, :n]), reads=[rsk], writes=[rsk])
            h32, h32k = R16.next()
            for kb in range(KB):
                P.op("pool", lambda e, h32=h32, rs=rs, kb=kb, n=n, l0=l0: e.tensor_tensor(out=h32[:, kb, :n], in0=xres[:, kb, l0:l0 + n], in1=rs[:, :n], op=ALU.mult),
                     reads=[("xres", c0), rsk, sqk], writes=[h32k])
                P.op("dve", lambda e, h32=h32, kb=kb, n=n, kind=kind: e.tensor_scalar(out=h32[:, kb, :n], in0=h32[:, kb, :n], scalar1=A[:, kb, kind:kind + 1],
                                                                                 scalar2=mv[:, 8 + kb, kind:kind + 1], op0=ALU.mult, op1=ALU.add),
                     reads=[h32k, "A", mvk], writes=[h32k])
            P.op("act", lambda e, h32=h32, n=n, l0=l0: e.copy(out=hT[:, :, l0:l0 + n], in_=h32[:, :, :n]), reads=[h32k], writes=[("hT", c0)])
            for t0 in range(0, n, 128):
                m = min(128, n - t0)
                pl, plk = psA.next()
                for kb in range(KB):
                    P.op("pe", lambda e, pl=pl, h32=h32, kb=kb, t0=t0, m=m: e.matmul(pl[:m, 0:NE], h32[:, kb, t0:t0 + m], rws[:, kb, :], start=(kb == 0), stop=(kb == KB - 1)),
                         reads=[h32k, "rws"], writes=[plk], accum=kb > 0)
                lg, lgk = t32.next()
                P.op("dve", lambda e, lg=lg, pl=pl, m=m: e.tensor_tensor(out=lg[:m, :], in0=pl[:m, 0:NE], in1=rbb[:m, :], op=ALU.add), reads=[plk, "rbb"], writes=[lgk])
                tp, tpk = t8.next()
                P.op("dve", lambda e, tp=tp, lg=lg, m=m: e.max(out=tp[:m, :], in_=lg[:m, :]), reads=[lgk], writes=[tpk])
                mk_, mkk = t32.next()
                P.op("dve", lambda e, mk_=mk_, lg=lg, tp=tp, m=m: e.tensor_scalar(out=mk_[:m, :], in0=lg[:m, :], scalar1=tp[:m, 3:4], scalar2=None, op0=ALU.is_ge), reads=[lgk, tpk], writes=[mkk])
                nm, nmk = t8.next()
                P.op("dve", lambda e, nm=nm, tp=tp, m=m: e.tensor_scalar(out=nm[:m, 0:1], in0=tp[:m, 0:1], scalar1=-1.0, scalar2=None, op0=ALU.mult), reads=[tpk], writes=[nmk])
                P.op("act", lambda e, lg=lg, nm=nm, m=m: e.activation(out=lg[:m, :], in_=lg[:m, :], func=AF.Exp, bias=nm[:m, 0:1], scale=1.0), reads=[lgk, nmk, mkk], writes=[lgk])
                P.op("dve", lambda e, lg=lg, mk_=mk_, m=m: e.tensor_tensor(out=lg[:m, :], in0=lg[:m, :], in1=mk_[:m, :], op=ALU.mult), reads=[lgk, mkk], writes=[lgk])
                P.op("dve", lambda e, nm=nm, lg=lg, m=m: e.tensor_reduce(out=nm[:m, 1:2], in_=lg[:m, :], axis=AX.X, op=ALU.add), reads=[lgk], writes=[nmk])
                P.op("dve", lambda e, nm=nm, m=m: e.reciprocal(out=nm[:m, 1:2], in_=nm[:m, 1:2]), reads=[nmk], writes=[nmk])
                P.op("dve", lambda e, lg=lg, nm=nm, m=m: e.tensor_scalar(out=lg[:m, :], in0=lg[:m, :], scalar1=nm[:m, 1:2], scalar2=None, op0=ALU.mult), reads=[lgk, nmk], writes=[lgk])
                pt2, pt2k = psA.next()
                P.op("pe", lambda e, pt2=pt2, lg=lg, m=m: e.transpose(out=pt2[0:NE, 0:m], in_=lg[:m, :], identity=idn[:m, :m]), reads=[lgk, "idn"], writes=[pt2k])
                P.op("act", lambda e, pt2=pt2, m=m, c=l0 + t0: e.copy(out=gwT[:, c:c + m], in_=pt2[0:NE, 0:m]), reads=[pt2k], writes=[("gwT", c0)])
        P.dma(x1o[:, lo:hi].rearrange("(kb p) t -> p kb t", p=128), xres[:, :, :hi - lo], reads=[("xres", c0) for c0, _ in chs], writes=[("x1o", gi)])
        P.dma(hTo[:, lo:hi].rearrange("(kb p) t -> p kb t", p=128), hT[:, :, :hi - lo], reads=[("hT", c0) for c0, _ in chs], writes=[("hTo", gi)])
        P.dma(gwo[:, lo:hi], gwT[:, :hi - lo], reads=[("gwT", c0) for c0, _ in chs], writes=[("gwo", gi)])
        P.barrier()
    return P.finish()


def build_p3(T, ctx0, odd, ngroups=2, n_exp=NE):
    P = Prog()
    xT = P.dram_in("xT", [D, T]); yT = P.dram_in("yT", [D, T]); cT = P.dram_in("cT", [D, 2])
    modw = P.dram_in("modw", [D, 4096]); modb = P.dram_in("modb", [4096]); g2n = P.dram_in("g", [D])
    wo = P.dram_in("wo", [D, D])
    rw = P.dram_in("rw", [D, NE]); rb = P.dram_in("rb", [1, NE])
    wg = P.dram_in("wg", [NE, D, D]); wu = P.dram_in("wu", [NE, D, D]); wd = P.dram_in("wd", [NE, D, D])
    bg = P.dram_in("bg", [NE, D]); bu = P.dram_in("bu", [NE, D]); bd = P.dram_in("bd", [NE, D])
    ident = P.dram_in("ident", [128, 128])
    if odd:
        lng = P.dram_in("lng", [D]); lnb = P.dram_in("lnb", [D]); b2 = P.dram_in("b2", [D])
    xo = P.dram_out("xo", [D, T])
    TG = (T + ngroups - 1) // ngroups
    psA = Rot(P, 2, [128, 512], F32, psum=True, tag="psA")
    psG = Rot(P, 2, [128, 512], F32, psum=True, tag="psG")
    psU = Rot(P, 2, [128, 512], F32, psum=True, tag="psU")
    psDn = Rot(P, 2, [128, 512], F32, psum=True, tag="psDn")
    R16 = Rot(P, 2, [128, KB, 512], F32, tag="R16")
    mv, mvk = adaln_vecs(P, cT, modw, modb, 4, psA, "ad", wbuf=R16)
    idn = P.sb([128, 128]); P.dma(idn[:], ident, writes=["idn"])
    gs = P.sb([128, KB]); P.dma(gs[:], g2n.rearrange("(kb p) -> p kb", p=128), writes=["gs"], allow_slow_non_contiguous=True)
    A = P.sb([128, KB, 2])
    P.op("dve", lambda e: e.tensor_scalar(out=A[:], in0=mv[:, 16:24, :], scalar1=1.0, scalar2=None, op0=ALU.add), reads=[mvk], writes=["A"])
    P.op("dve", lambda e: e.tensor_tensor(out=A[:], in0=A[:], in1=gs[:].unsqueeze(2).to_broadcast([128, KB, 2]), op=ALU.mult), reads=["A", "gs"], writes=["A"])
    ones = P.sb([128, 128]); P.op("pool", lambda e: e.memset(ones[:], 1.0 / D), writes=["ones"])
    if odd:
        lgs = P.sb([128, KB]); lbs = P.sb([128, KB]); b2s = P.sb([128, KB])
        P.dma(lgs[:], lng.rearrange("(kb p) -> p kb", p=128), writes=["lgs"], allow_slow_non_contiguous=True)
        P.dma(lbs[:], lnb.rearrange("(kb p) -> p kb", p=128), writes=["lbs"], allow_slow_non_contiguous=True)
        P.dma(b2s[:], b2.rearrange("(kb p) -> p kb", p=128), writes=["b2s"], allow_slow_non_contiguous=True)
    braw = P.sb([32, 3, D])
    P.dma(braw[:, 0, :], bg, writes=["braw"]); P.dma(braw[:, 1, :], bu, writes=["braw"]); P.dma(braw[:, 2, :], bd, writes=["braw"])
    bgT = P.sb([128, KB, NE]); bg17 = P.sb([128, KB, NE]); bu1 = P.sb([128, KB, NE])
    for which, dst in ((0, bgT), (1, bu1)):
        pt, pk = psA.next()
        for fb in range(KB):
            P.op("pe", lambda e, pt=pt, fb=fb, which=which: e.transpose(out=pt[:, fb * 32:(fb + 1) * 32], in_=braw[:, which, fb * 128:(fb + 1) * 128], identity=idn[0:32, 0:32]),
                 reads=["braw", "idn"], writes=[pk], accum=fb > 0)
        P.op("dve", lambda e, pt=pt, dst=dst: e.tensor_copy(out=dst[:].rearrange("p a b -> p (a b)"), in_=pt[:, 0:KB * 32]), reads=[pk], writes=["bias"])
    P.op("dve", lambda e: e.tensor_scalar(out=bg17[:], in0=bgT[:], scalar1=1.702, scalar2=None, op0=ALU.mult), reads=["bias"], writes=["bias"])
    P.op("dve", lambda e: e.tensor_scalar(out=bu1[:], in0=bu1[:], scalar1=1.0, scalar2=None, op0=ALU.add), reads=["bias"], writes=["bias"])
    bdb = P.sb([32, D], BF16)
    P.op("act", lambda e: e.copy(out=bdb[:], in_=braw[:, 2, :]), reads=["braw"], writes=["bdb"])
    rws = P.sb([128, KB, NE]); rbb = P.sb([128, NE])
    P.dma(rws[:], rw.rearrange("(kb p) n -> p kb n", p=128), writes=["rws"])
    P.dma(rbb[:], rb.partition_broadcast(128), writes=["rbb"])
    wos = P.sb([128, KB, D], BF16)
    for kb in range(KB):
        P.dma(wos[:, kb, :], wo[kb * 128:(kb + 1) * 128, :], writes=[("wos", kb)], queue="pool")
    xres = P.sb([128, KB, TG]); hT = P.sb([128, KB, TG], BF16)
    gwT = P.sb([32, TG]); gwTb = P.sb([32, TG], BF16)
    R8 = Rot(P, 2, [128, KB, 512], BF16, tag="R8")
    t512 = Rot(P, 6, [128, 512], F32, tag="t512")
    gwb = Rot(P, 2, [128, TG], F32, tag="gwb")
    aTr = Rot(P, 2, [128, 2, 512], BF16, tag="aT")
    wgr = Rot(P, 3, [128, KB, 256], BF16, tag="wg"); wur = Rot(P, 3, [128, KB, 256], BF16, tag="wu"); wdr = Rot(P, 3, [128, 2, D], BF16, tag="wd")
    selr = Rot(P, 2, [32, 128], F32, tag="sel")
    t32 = Rot(P, 6, [128, NE], F32, tag="t32"); t8 = Rot(P, 3, [128, 8], F32, tag="t8")

    for gi in range(ngroups):
        lo, hi = gi * TG, min(T, (gi + 1) * TG)
        chs = split_chunks(lo, hi, ctx0)
        P.dma(xres[:, :, :hi - lo], xT[:, lo:hi].rearrange("(kb p) t -> p kb t", p=128), writes=[("xres", c0) for c0, _ in chs])
        for (c0, n) in chs:
            kind = 1 if c0 >= ctx0 else 0
            l0 = c0 - lo
            yt, ytk = R16.next()
            P.dma(yt[:, :, :n], yT[:, c0:c0 + n].rearrange("(kb p) t -> p kb t", p=128), writes=[ytk])
            yb, ybk = R8.next()
            if not odd:
                P.op("act", lambda e, yb=yb, yt=yt, n=n: e.copy(out=yb[:, :, :n], in_=yt[:, :, :n]), reads=[ytk], writes=[ybk])
            else:
                sq, sqk = R16.next()
                P.op("act", lambda e, sq=sq, yt=yt, n=n: e.activation(out=sq[:, :, :n], in_=yt[:, :, :n], func=AF.Square), reads=[ytk], writes=[sqk])
                pm, pmk = psA.next(); pq, pqk = psA.next()
                for kb in range(KB):
                    P.op("pe", lambda e, pm=pm, yt=yt, kb=kb, n=n: e.matmul(pm[:, :n], ones[:], yt[:, kb, :n], start=(kb == 0), stop=(kb == KB - 1)),
                         reads=[ytk, "ones"], writes=[pmk], accum=kb > 0)
                for kb in range(KB):
                    P.op("pe", lambda e, pq=pq, sq=sq, kb=kb, n=n: e.matmul(pq[:, :n], ones[:], sq[:, kb, :n], start=(kb == 0), stop=(kb == KB - 1)),
                         reads=[sqk, "ones"], writes=[pqk], accum=kb > 0)
                mean, meank = t512.next(); var, vark = t512.next()
                P.op("act", lambda e, mean=mean, pm=pm, n=n: e.copy(out=mean[:, :n], in_=pm[:, :n]), reads=[pmk], writes=[meank])
                P.op("dve", lambda e, var=var, mean=mean, n=n: e.tensor_tensor(out=var[:, :n], in0=mean[:, :n], in1=mean[:, :n], op=ALU.mult), reads=[meank], writes=[vark])
                P.op("dve", lambda e, var=var, pq=pq, n=n: e.tensor_tensor(out=var[:, :n], in0=pq[:, :n], in1=var[:, :n], op=ALU.subtract), reads=[pqk, vark], writes=[vark])
                P.op("act", lambda e, var=var, n=n: e.activation(out=var[:, :n], in_=var[:, :n], func=AF.Sqrt, bias=EPS, scale=1.0), reads=[vark], writes=[vark])
                P.op("dve", lambda e, var=var, n=n: e.reciprocal(out=var[:, :n], in_=var[:, :n]), reads=[vark], writes=[vark])
                for kb in range(KB):
                    P.op("dve", lambda e, yt=yt, mean=mean, kb=kb, n=n: e.tensor_tensor(out=yt[:, kb, :n], in0=yt[:, kb, :n], in1=mean[:, :n], op=ALU.subtract), reads=[ytk, meank], writes=[ytk])
                    P.op("pool", lambda e, yt=yt, var=var, kb=kb, n=n: e.tensor_tensor(out=yt[:, kb, :n], in0=yt[:, kb, :n], in1=var[:, :n], op=ALU.mult), reads=[ytk, vark], writes=[ytk])
                    P.op("act", lambda e, yt=yt, yb=yb, kb=kb, n=n: e.activation(out=yb[:, kb, :n], in_=yt[:, kb, :n], func=AF.Silu, scale=lgs[:, kb:kb + 1], bias=lbs[:, kb:kb + 1]),
                         reads=[ytk, "lgs", "lbs"], writes=[ybk])
            for db in range(KB):
                pt, pk = psDn.next()
                for kb in range(KB):
                    P.op("pe", lambda e, pt=pt, yb=yb, db=db, kb=kb, n=n: e.matmul(pt[:, :n], wos[:, kb, db * 128:(db + 1) * 128], yb[:, kb, :n], start=(kb == 0), stop=(kb == KB - 1)),
                         reads=[("wos", kb), ybk], writes=[pk], accum=kb > 0)
                if odd:
                    tb, tbk = t512.next()
                    P.op("act", lambda e, tb=tb, pt=pt, db=db, n=n: e.activation(out=tb[:, :n], in_=pt[:, :n], func=AF.Identity, bias=b2s[:, db:db + 1], scale=1.0), reads=[pk, "b2s"], writes=[tbk])
                    src, srck = tb, tbk
                else:
                    src, srck = pt, pk
                P.op("dve", lambda e, src=src, db=db, n=n, l0=l0, kind=kind: e.scalar_tensor_tensor(
                    out=xres[:, db, l0:l0 + n], in0=src[:, :n], scalar=mv[:, db, kind:kind + 1], in1=xres[:, db, l0:l0 + n], op0=ALU.mult, op1=ALU.add),
                    reads=[srck, mvk, ("xres", c0)], writes=[("xres", c0)])
        for (c0, n) in chs:
            kind = 1 if c0 >= ctx0 else 0
            l0 = c0 - lo
            sq, sqk = R16.next()
            P.op("act", lambda e, sq=sq, n=n, l0=l0: e.activation(out=sq[:, :, :n], in_=xres[:, :, l0:l0 + n], func=AF.Square), reads=[("xres", c0)], writes=[sqk])
            pt, pk = psA.next()
            for kb in range(KB):
                P.op("pe", lambda e, pt=pt, sq=sq, kb=kb, n=n: e.matmul(pt[:, :n], ones[:], sq[:, kb, :n], start=(kb == 0), stop=(kb == KB - 1)),
                     reads=[sqk, "ones"], writes=[pk], accum=kb > 0)
            rs, rsk = t512.next()
            P.op("act", lambda e, rs=rs, pt=pt, n=n: e.activation(out=rs[:, :n], in_=pt[:, :n], func=AF.Sqrt, bias=EPS, scale=1.0), reads=[pk], writes=[rsk])
            P.op("dve", lambda e, rs=rs, n=n: e.reciprocal(out=rs[:, :n], in_=rs[:, :n]), reads=[rsk], writes=[rsk])
            h32, h32k = R16.next()
            for kb in range(KB):
                P.op("pool", lambda e, h32=h32, rs=rs, kb=kb, n=n, l0=l0: e.tensor_tensor(out=h32[:, kb, :n], in0=xres[:, kb, l0:l0 + n], in1=rs[:, :n], op=ALU.mult),
                     reads=[("xres", c0), rsk, sqk], writes=[h32k])
                P.op("dve", lambda e, h32=h32, kb=kb, n=n, kind=kind: e.tensor_scalar(out=h32[:, kb, :n], in0=h32[:, kb, :n], scalar1=A[:, kb, kind:kind + 1],
                                                                                 scalar2=mv[:, 8 + kb, kind:kind + 1], op0=ALU.mult, op1=ALU.add),
                     reads=[h32k, "A", mvk], writes=[h32k])
            P.op("act", lambda e, h32=h32, n=n, l0=l0: e.copy(out=hT[:, :, l0:l0 + n], in_=h32[:, :, :n]), reads=[h32k], writes=[("hT", c0)])
            for t0 in range(0, n, 128):
                m = min(128, n - t0)
                pl, plk = psA.next()
                for kb in range(KB):
                    P.op("pe", lambda e, pl=pl, h32=h32, kb=kb, t0=t0, m=m: e.matmul(pl[:m, 0:NE], h32[:, kb, t0:t0 + m], rws[:, kb, :], start=(kb == 0), stop=(kb == KB - 1)),
                         reads=[h32k, "rws"], writes=[plk], accum=kb > 0)
                lg, lgk = t32.next()
                P.op("dve", lambda e, lg=lg, pl=pl, m=m: e.tensor_tensor(out=lg[:m, :], in0=pl[:m, 0:NE], in1=rbb[:m, :], op=ALU.add), reads=[plk, "rbb"], writes=[lgk])
                tp, tpk = t8.next()
                P.op("dve", lambda e, tp=tp, lg=lg, m=m: e.max(out=tp[:m, :], in_=lg[:m, :]), reads=[lgk], writes=[tpk])
                mk_, mkk = t32.next()
                P.op("dve", lambda e, mk_=mk_, lg=lg, tp=tp, m=m: e.tensor_scalar(out=mk_[:m, :], in0=lg[:m, :], scalar1=tp[:m, 3:4], scalar2=None, op0=ALU.is_ge), reads=[lgk, tpk], writes=[mkk])
                nm, nmk = t8.next()
                P.op("dve", lambda e, nm=nm, tp=tp, m=m: e.tensor_scalar(out=nm[:m, 0:1], in0=tp[:m, 0:1], scalar1=-1.0, scalar2=None, op0=ALU.mult), reads=[tpk], writes=[nmk])
                P.op("act", lambda e, lg=lg, nm=nm, m=m: e.activation(out=lg[:m, :], in_=lg[:m, :], func=AF.Exp, bias=nm[:m, 0:1], scale=1.0), reads=[lgk, nmk, mkk], writes=[lgk])
                P.op("dve", lambda e, lg=lg, mk_=mk_, m=m: e.tensor_tensor(out=lg[:m, :], in0=lg[:m, :], in1=mk_[:m, :], op=ALU.mult), reads=[lgk, mkk], writes=[lgk])
                P.op("dve", lambda e, nm=nm, lg=lg, m=m: e.tensor_reduce(out=nm[:m, 1:2], in_=lg[:m, :], axis=AX.X, op=ALU.add), reads=[lgk], writes=[nmk])
                P.op("dve", lambda e, nm=nm, m=m: e.reciprocal(out=nm[:m, 1:2], in_=nm[:m, 1:2]), reads=[nmk], writes=[nmk])
                P.op("dve", lambda e, lg=lg, nm=nm, m=m: e.tensor_scalar(out=lg[:m, :], in0=lg[:m, :], scalar1=nm[:m, 1:2], scalar2=None, op0=ALU.mult), reads=[lgk, nmk], writes=[lgk])
                pt2, pt2k = psA.next()
                P.op("pe", lambda e, pt2=pt2, lg=lg, m=m: e.transpose(out=pt2[0:NE, 0:m], in_=lg[:m, :], identity=idn[:m, :m]), reads=[lgk, "idn"], writes=[pt2k])
                P.op("act", lambda e, pt2=pt2, m=m, c=l0 + t0: e.copy(out=gwT[:, c:c + m], in_=pt2[0:NE, 0:m]), reads=[pt2k], writes=[("gwT", c0)])
                P.op("dve", lambda e, pt2=pt2, m=m, c=l0 + t0: e.tensor_copy(out=gwTb[:, c:c + m], in_=pt2[0:NE, 0:m]), reads=[pt2k], writes=[("gwTb", c0)])
        first_unit = True
        for ex in range(n_exp):
            sel, selk = selr.next()
            P.op("dve", lambda e, sel=sel, ex=ex: e.tensor_copy(out=sel[:], in_=idn[0:32, ex:ex + 1].to_broadcast([32, 128])), reads=["idn"], writes=[selk])
            gb, gbk = gwb.next()
            for (c0, n) in chs:
                l0 = c0 - lo
                pt, pk = psA.next()
                P.op("pe", lambda e, pt=pt, sel=sel, l0=l0, n=n: e.matmul(pt[:, :n], sel[:], gwT[:, l0:l0 + n], start=True, stop=True), reads=[selk, ("gwT", c0)], writes=[pk])
                P.op("act", lambda e, gb=gb, pt=pt, l0=l0, n=n: e.copy(out=gb[:, l0:l0 + n], in_=pt[:, :n]), reads=[pk], writes=[(gbk, c0)])
            for qd in range(4):
                f0 = qd * 256
                wgt, wgk = wgr.next(); wut, wuk = wur.next(); wdt, wdk = wdr.next()
                P.dma(wgt[:], wg[ex][:, f0:f0 + 256].rearrange("(kb p) f -> p kb f", p=128), writes=[wgk], queue="pool")
                P.dma(wut[:], wu[ex][:, f0:f0 + 256].rearrange("(kb p) f -> p kb f", p=128), writes=[wuk], queue="pool")
                P.dma(wdt[:], wd[ex][f0:f0 + 256, :].rearrange("(fb p) d -> p fb d", p=128), writes=[wdk], queue="pool")
                for (c0, n) in chs:
                    kind = 1 if c0 >= ctx0 else 0
                    l0 = c0 - lo
                    at, atk = aTr.next()
                    for fb in range(2):
                        fa = qd * 2 + fb
                        pg, pgk = psG.next(); pu, puk = psU.next()
                        for kb in range(KB):
                            P.op("pe", lambda e, pg=pg, wgt=wgt, kb=kb, fb=fb, l0=l0, n=n: e.matmul(pg[:, :n], wgt[:, kb, fb * 128:(fb + 1) * 128], hT[:, kb, l0:l0 + n], start=(kb == 0), stop=(kb == KB - 1)),
                                 reads=[wgk, ("hT", c0)], writes=[pgk], accum=kb > 0)
                        for kb in range(KB):
                            P.op("pe", lambda e, pu=pu, wut=wut, kb=kb, fb=fb, l0=l0, n=n: e.matmul(pu[:, :n], wut[:, kb, fb * 128:(fb + 1) * 128], hT[:, kb, l0:l0 + n], start=(kb == 0), stop=(kb == KB - 1)),
                                 reads=[wuk, ("hT", c0)], writes=[puk], accum=kb > 0)
                        t1, t1k = t512.next(); t2, t2k = t512.next(); t3, t3k = t512.next()
                        P.op("dve", lambda e, t1=t1, pg=pg, fa=fa, ex=ex, n=n: e.tensor_scalar(out=t1[:, :n], in0=pg[:, :n], scalar1=bgT[:, fa, ex:ex + 1], scalar2=7.0, op0=ALU.add, op1=ALU.min),
                             reads=[pgk, "bias"], writes=[t1k])
                        P.op("act", lambda e, t2=t2, t1=t1, n=n: e.activation(out=t2[:, :n], in_=t1[:, :n], func=AF.Sigmoid, scale=1.702),
                             reads=[t1k], writes=[t2k])
                        P.op("dve", lambda e, t3=t3, pu=pu, fa=fa, ex=ex, n=n: e.tensor_scalar(out=t3[:, :n], in0=pu[:, :n], scalar1=bu1[:, fa, ex:ex + 1], scalar2=8.0, op0=ALU.add, op1=ALU.min),
                             reads=[puk, "bias"], writes=[t3k])
                        P.op("dve", lambda e, t2=t2, t1=t1, n=n: e.scalar_tensor_tensor(out=t2[:, :n], in0=t2[:, :n], scalar=SIG_CLAMP, in1=t1[:, :n], op0=ALU.min, op1=ALU.mult),
                             reads=[t1k, t2k], writes=[t2k])
                        P.op("dve", lambda e, t3=t3, t2=t2, n=n: e.scalar_tensor_tensor(out=t3[:, :n], in0=t3[:, :n], scalar=-6.0, in1=t2[:, :n], op0=ALU.max, op1=ALU.mult),
                             reads=[t2k, t3k], writes=[t3k])
                        P.op("pool", lambda e, at=at, t3=t3, gb=gb, fb=fb, l0=l0, n=n: e.tensor_tensor(out=at[:, fb, :n], in0=t3[:, :n], in1=gb[:, l0:l0 + n], op=ALU.mult),
                             reads=[t3k, (gbk, c0)], writes=[(atk, fb)])
                    for db in range(KB):
                        pt, pk = psDn.next()
                        for fb in range(2):
                            P.op("pe", lambda e, pt=pt, wdt=wdt, at=at, fb=fb, db=db, n=n, fu=first_unit: e.matmul(pt[:, :n], wdt[:, fb, db * 128:(db + 1) * 128], at[:, fb, :n], start=(fb == 0),
                                                                                           stop=(fb == 1 and not fu)),
                                 reads=[wdk, (atk, fb)], writes=[pk], accum=fb > 0)
                        if first_unit:
                            P.op("pe", lambda e, pt=pt, db=db, l0=l0, n=n: e.matmul(pt[:, :n], bdb[:, db * 128:(db + 1) * 128], gwTb[:, l0:l0 + n], start=False, stop=True),
                                 reads=["bdb", ("gwTb", c0)], writes=[pk], accum=True)
                        P.op("dve", lambda e, pt=pt, db=db, n=n, l0=l0, kind=kind: e.scalar_tensor_tensor(
                            out=xres[:, db, l0:l0 + n], in0=pt[:, :n], scalar=mv[:, 24 + db, kind:kind + 1], in1=xres[:, db, l0:l0 + n], op0=ALU.mult, op1=ALU.add),
                            reads=[pk, mvk, ("xres", c0)], writes=[("xres", c0)])
                first_unit = False
        P.dma(xo[:, lo:hi].rearrange("(kb p) t -> p kb t", p=128), xres[:, :, :hi - lo], reads=[("xres", c0) for c0, _ in chs], writes=[("xo", gi)])
        P.barrier()
    return P.finish()


Q_SEGS = ((0, 2048), (2080, 128))
Q_T = 2240


def build_q1():
    P = Prog()
    T = Q_T
    ctx0 = 2080
    xT = P.dram_in("xT", [D, T]); valid = P.dram_in("valid", [1, T]); cT = P.dram_in("cT", [D, 2])
    modw = P.dram_in("modw", [D, 2048]); modb = P.dram_in("modb", [2048]); g1 = P.dram_in("g", [D])
    w1 = P.dram_in("w1", [D, 2048]); b1 = P.dram_in("b1", [2048])
    dww = P.dram_in("dww", [31, D]); dwb = P.dram_in("dwb", [D]); ident = P.dram_in("ident", [128, 128])
    cv = P.dram_out("cv", [D, 2176])
    psA = Rot(P, 2, [128, 512], F32, psum=True, tag="psA")
    psG = Rot(P, 2, [128, 512], F32, psum=True, tag="psG")
    psU = Rot(P, 2, [128, 512], F32, psum=True, tag="psU")
    xbuf = Rot(P, 2, [128, KB, 512], F32, tag="x")
    mv, mvk = adaln_vecs(P, cT, modw, modb, 2, psA, "ad", wbuf=xbuf)
    idn = P.sb([128, 128]); P.dma(idn[:], ident, writes=["idn"])
    gs = P.sb([128, KB]); P.dma(gs[:], g1.rearrange("(kb p) -> p kb", p=128), writes=["gs"], allow_slow_non_contiguous=True)
    b1s = P.sb([128, 16]); P.dma(b1s[:], b1.rearrange("(nb p) -> p nb", p=128), writes=["b1s"], allow_slow_non_contiguous=True)
    dbs = P.sb([128, KB]); P.dma(dbs[:], dwb.rearrange("(nb p) -> p nb", p=128), writes=["dbs"], allow_slow_non_contiguous=True)
    A = P.sb([128, KB, 2])
    P.op("dve", lambda e: e.tensor_scalar(out=A[:], in0=mv[:, 8:16, :], scalar1=1.0, scalar2=None, op0=ALU.add), reads=[mvk], writes=["A"])
    P.op("dve", lambda e: e.tensor_tensor(out=A[:], in0=A[:], in1=gs[:].unsqueeze(2).to_broadcast([128, KB, 2]), op=ALU.mult), reads=["A", "gs"], writes=["A"])
    ones = P.sb([128, 128]); P.op("pool", lambda e: e.memset(ones[:], 1.0 / D), writes=["ones"])
    dwraw = P.sb([31, D]); P.dma(dwraw[:], dww, writes=["dwraw"])
    dwT = P.sb([128, KB, 32])
    pt, pk = psA.next()
    for cb in range(KB):
        P.op("pe", lambda e, pt=pt, cb=cb: e.transpose(out=pt[:, cb * 32:cb * 32 + 31], in_=dwraw[:, cb * 128:(cb + 1) * 128], identity=idn[0:31, 0:31]),
             reads=["dwraw", "idn"], writes=[pk], accum=cb > 0)
    P.op("dve", lambda e, pt=pt: e.tensor_copy(out=dwT[:, :, 0:31], in_=pt[:, 0:KB * 32].rearrange("p (a b) -> p a b", b=32)[:, :, 0:31]), reads=[pk], writes=["dwT"])
    vb = P.sb([128, T]); P.dma(vb[:], valid.partition_broadcast(128), writes=["vb"])
    hT = P.sb([128, KB, T], BF16)
    sqb = Rot(P, 1, [128, KB, 512], F32, tag="sq")
    t512 = Rot(P, 6, [128, 512], F32, tag="t512")
    rsb = Rot(P, 2, [128, 512], F32, tag="rsb")
    chs = split_chunks(0, T, ctx0)
    for (c0, n) in chs:
        kind = 1 if c0 >= ctx0 else 0
        xt, xk = xbuf.next()
        P.dma(xt[:, :, :n], xT[:, c0:c0 + n].rearrange("(kb p) t -> p kb t", p=128), writes=[xk])
        sq, sqk = sqb.next()
        P.op("act", lambda e, sq=sq, xt=xt, n=n: e.activation(out=sq[:, :, :n], in_=xt[:, :, :n], func=AF.Square), reads=[xk], writes=[sqk])
        pt, pk = psA.next()
        for kb in range(KB):
            P.op("pe", lambda e, pt=pt, sq=sq, kb=kb, n=n: e.matmul(pt[:, :n], ones[:], sq[:, kb, :n], start=(kb == 0), stop=(kb == KB - 1)),
                 reads=[sqk, "ones"], writes=[pk], accum=kb > 0)
        rs, rsk = rsb.next()
        P.op("act", lambda e, rs=rs, pt=pt, n=n: e.activation(out=rs[:, :n], in_=pt[:, :n], func=AF.Sqrt, bias=EPS, scale=1.0), reads=[pk], writes=[rsk])
        P.op("dve", lambda e, rs=rs, n=n: e.reciprocal(out=rs[:, :n], in_=rs[:, :n]), reads=[rsk], writes=[rsk])
        for kb in range(KB):
            tp, tk = t512.next()
            P.op("pool", lambda e, tp=tp, xt=xt, rs=rs, kb=kb, n=n: e.tensor_tensor(out=tp[:, :n], in0=xt[:, kb, :n], in1=rs[:, :n], op=ALU.mult), reads=[xk, rsk], writes=[tk])
            P.op("dve", lambda e, tp=tp, kb=kb, n=n, c0=c0, kind=kind: e.tensor_scalar(out=hT[:, kb, c0:c0 + n], in0=tp[:, :n], scalar1=A[:, kb, kind:kind + 1],
                                                                               scalar2=mv[:, kb, kind:kind + 1], op0=ALU.mult, op1=ALU.add),
                 reads=[tk, "A", mvk], writes=[("hT", c0)])
    war = Rot(P, 2, [128, KB, 128], BF16, tag="wa"); wgr = Rot(P, 2, [128, KB, 128], BF16, tag="wgl")
    aer = Rot(P, 2, [128, T], F32, tag="aext")
    accD = Rot(P, 2, [128, 2176], F32, tag="accD"); accP = Rot(P, 2, [128, 2176], F32, tag="accP")
    ptmp = Rot(P, 2, [128, 2048], F32, tag="ptmp")
    for cb in range(KB):
        wa, wak = war.next(); wgt, wgk = wgr.next()
        P.dma(wa[:], w1[:, cb * 128:(cb + 1) * 128].rearrange("(kb p) f -> p kb f", p=128), writes=[wak], queue="pool")
        P.dma(wgt[:], w1[:, 1024 + cb * 128:1024 + (cb + 1) * 128].rearrange("(kb p) f -> p kb f", p=128), writes=[wgk], queue="pool")
        ae, aek = aer.next()
        for (c0, n) in chs:
            pa, pak = psG.next(); pg, pgk = psU.next()
            for kb in range(KB):
                P.op("pe", lambda e, pa=pa, wa=wa, kb=kb, c0=c0, n=n: e.matmul(pa[:, :n], wa[:, kb, :], hT[:, kb, c0:c0 + n], start=(kb == 0), stop=(kb == KB - 1)),
                     reads=[wak, ("hT", c0)], writes=[pak], accum=kb > 0)
            for kb in range(KB):
                P.op("pe", lambda e, pg=pg, wgt=wgt, kb=kb, c0=c0, n=n: e.matmul(pg[:, :n], wgt[:, kb, :], hT[:, kb, c0:c0 + n], start=(kb == 0), stop=(kb == KB - 1)),
                     reads=[wgk, ("hT", c0)], writes=[pgk], accum=kb > 0)
            sg, sgk = t512.next()
            P.op("act", lambda e, sg=sg, pg=pg, cb=cb, n=n: e.activation(out=sg[:, :n], in_=pg[:, :n], func=AF.Sigmoid, bias=b1s[:, 8 + cb:9 + cb], scale=1.0),
                 reads=[pgk, "b1s"], writes=[sgk])
            P.op("dve", lambda e, sg=sg, pa=pa, cb=cb, n=n: e.scalar_tensor_tensor(out=sg[:, :n], in0=pa[:, :n], scalar=b1s[:, cb:cb + 1], in1=sg[:, :n], op0=ALU.add, op1=ALU.mult),
                 reads=[pak, sgk, "b1s"], writes=[sgk])
            P.op("pool", lambda e, ae=ae, sg=sg, c0=c0, n=n: e.tensor_tensor(out=ae[:, c0:c0 + n], in0=sg[:, :n], in1=vb[:, c0:c0 + n], op=ALU.mult),
                 reads=[sgk, "vb"], writes=[aek])
        ad, adk = accD.next(); ap_, apk = accP.next()
        for (base, nown), oc in zip(Q_SEGS, (0, 2048)):
            o = slice(oc, oc + nown)
            P.op("dve", lambda e, ad=ad, ae=ae, o=o, base=base, nown=nown, cb=cb: e.tensor_scalar(
                out=ad[:, o], in0=ae[:, base + 1:base + 1 + nown], scalar1=dwT[:, cb, 0:1], scalar2=dbs[:, cb:cb + 1], op0=ALU.mult, op1=ALU.add),
                reads=[aek, "dwT", "dbs"], writes=[adk])
            NDVE = 20
            for k in range(1, NDVE):
                P.op("dve", lambda e, ad=ad, ae=ae, o=o, base=base, nown=nown, cb=cb, k=k: e.scalar_tensor_tensor(
                    out=ad[:, o], in0=ae[:, base + 1 + k:base + 1 + k + nown], scalar=dwT[:, cb, k:k + 1], in1=ad[:, o], op0=ALU.mult, op1=ALU.add),
                    reads=[aek, "dwT", adk], writes=[adk])
            P.op("pool", lambda e, ap_=ap_, ae=ae, o=o, base=base, nown=nown, cb=cb: e.tensor_scalar(
                out=ap_[:, o], in0=ae[:, base + 1 + NDVE:base + 1 + NDVE + nown], scalar1=dwT[:, cb, NDVE:NDVE + 1], scalar2=0.0, op0=ALU.mult, op1=ALU.add),
                reads=[aek, "dwT"], writes=[apk])
            for k in range(NDVE + 1, 31):
                tp_, tpk_ = ptmp.next()
                P.op("pool", lambda e, tp_=tp_, ae=ae, base=base, nown=nown, cb=cb, k=k: e.tensor_scalar(
                    out=tp_[:, :nown], in0=ae[:, base + 1 + k:base + 1 + k + nown], scalar1=dwT[:, cb, k:k + 1], scalar2=0.0, op0=ALU.mult, op1=ALU.add),
                    reads=[aek, "dwT"], writes=[tpk_])
                P.op("pool", lambda e, ap_=ap_, tp_=tp_, o=o, nown=nown: e.tensor_tensor(out=ap_[:, o], in0=ap_[:, o], in1=tp_[:, :nown], op=ALU.add),
                     reads=[tpk_, apk], writes=[apk])
        P.op("dve", lambda e, ad=ad, ap_=ap_: e.tensor_tensor(out=ad[:], in0=ad[:], in1=ap_[:], op=ALU.add), reads=[adk, apk], writes=[adk])
        P.dma(cv[cb * 128:(cb + 1) * 128, :], ad[:], reads=[adk], writes=[("cv", cb)])
    return P.finish()


_PROGS = {}
_CONSTS = {}


def _prog(name):
    if name not in _PROGS:
        if name == "p1":
            _PROGS[name] = build_p1(2176, 2048)
        elif name == "hy":
            _PROGS[name] = build_hy()
        elif name == "att":
            _PROGS[name] = build_att()
        elif name == "p3e":
            _PROGS[name] = build_p3a(2176, 2048, False)
        elif name == "p3o":
            _PROGS[name] = build_p3a(2176, 2048, True)
        elif name == "me":
            _PROGS[name] = build_me()
        elif name == "p3c":
            _PROGS[name] = build_p3c(2176)
        elif name == "q1":
            _PROGS[name] = build_q1()
    return _PROGS[name]


def _consts():
    if not _CONSTS:
        for s, L in (("l", 4096), ("c", 256)):
            Wc, Ws, Vc, Vs = dft_consts(L)
            featsT, decay = hyena_feats(L)
            _CONSTS[s] = dict(Wc=Wc, Ws=Ws, Vc=Vc, Vs=Vs, feats=featsT, decay=decay)
        C, S = rope_tables()
        mL, mR = attn_masks()
        m0 = np.ones((128, 1), np.float32)
        m0[0] = 0
        _CONSTS["misc"] = dict(ropeC=C, ropeS=S, maskL=mL, maskR=mR, ident=np.eye(128, dtype=np.float32), m0=m0)
    return _CONSTS


def _launch(name, in_maps):
    res = run_bass_kernel_spmd(_prog(name), in_maps, core_ids=list(range(NCORES)))
    return res.results


def _ca(a):
    return np.ascontiguousarray(a, dtype=np.float32)


def _tok_T(xl, xc, i):
    b, hf = i // 2, i % 2
    return _ca(np.concatenate([xl[b, hf * 2048:(hf + 1) * 2048], xc[b, hf * 128:(hf + 1) * 128]], 0).T)


def _scatter_tok(res_key, results, xl, xc):
    for i in range(NCORES):
        b, hf = i // 2, i % 2
        o = results[i][res_key]
        xl[b, hf * 2048:(hf + 1) * 2048] = o[:, :2048].T
        xc[b, hf * 128:(hf + 1) * 128] = o[:, 2048:].T


def kernel(x, c, ctx, c_ctx, mod_w, mod_b, norm1_g, norm2_g, ev_w_in, ev_w_out,
           hy_conv_w, hy_conv_b, hy_w1, hy_b1, hy_w2, hy_b2, hy_w3, hy_b3, hy_w_out, hy_freq, hy_skip,
           q_norm_g, k_norm_g, attn_sink,
           cf_w1, cf_b1, cf_dw_w, cf_dw_b, cf_ln_g, cf_ln_b, cf_w2, cf_b2,
           moe_router_w, moe_router_b, moe_w_gate, moe_b_gate, moe_w_up, moe_b_up, moe_w_down, moe_b_down):
    K = _consts()
    misc = K["misc"]
    f = lambda a: np.asarray(a, dtype=np.float32)
    xl = np.array(x, dtype=np.float32, copy=True)
    xc = np.array(ctx, dtype=np.float32, copy=True)
    c = f(c); c_ctx = f(c_ctx)
    cTs = [_ca(np.stack([c[i // 2], c_ctx], 0).T) for i in range(NCORES)]
    depth = mod_w.shape[0]
    for l in range(depth):
        mw, mb = f(mod_w[l]), f(mod_b[l])
        moe_in = dict(modw=_ca(mw[:, 2048:]), modb=_ca(mb[2048:]), g=f(norm2_g[l]), rw=f(moe_router_w[l]), rb=_ca(f(moe_router_b[l])[None]),
                      ident=misc["ident"])

        def back_half(name, maps):
            ra_ = _launch(name, maps)
            hall = np.concatenate([ra_[i]["hT"] for i in range(NCORES)], 1)
            gall = np.concatenate([ra_[i]["gw"] for i in range(NCORES)], 1)
            me_maps = []
            for i in range(NCORES):
                e0 = i * ME_NE
                me_maps.append(dict(hT=hall, gw=_ca(gall[e0:e0 + ME_NE]), wg=f(moe_w_gate[l][e0:e0 + ME_NE]), wu=f(moe_w_up[l][e0:e0 + ME_NE]),
                                    wd=f(moe_w_down[l][e0:e0 + ME_NE]), bg=f(moe_b_gate[l][e0:e0 + ME_NE]), bu=f(moe_b_up[l][e0:e0 + ME_NE]),
                                    bd=f(moe_b_down[l][e0:e0 + ME_NE]), ident=misc["ident"]))
            rm = _launch("me", me_maps)
            c_maps = []
            for i in range(NCORES):
                yp = _ca(np.stack([rm[j]["y"][:, i * 2176:(i + 1) * 2176] for j in range(NCORES)], 0))
                c_maps.append(dict(x1=ra_[i]["x1"], yp=yp, cT=cTs[i], modw=_ca(mw[:, 5120:]), modb=_ca(mb[5120:])))
            return _launch("p3c", c_maps)
        if l % 2 == 0:
            e = l // 2
            xTs = [_tok_T(xl, xc, i) for i in range(NCORES)]
            r = _launch("p1", [dict(xT=xTs[i], cT=cTs[i], modw=_ca(mw[:, :2048]), modb=_ca(mb[:2048]), g=f(norm1_g[l]), win=f(ev_w_in[e])) for i in range(NCORES)])
            p = np.zeros((4, 4096, 2304), np.float32)
            pc = np.zeros((4, 256, 2304), np.float32)
            _scatter_tok("pT", r, p, pc)
            cw_, cb_ = f(hy_conv_w[e]), f(hy_conv_b[e])
            fbv = _ca(np.stack([f(hy_b1[e]), f(hy_b2[e]), f(hy_b3[e]), f(hy_freq[e])], 1))
            hy_maps, att_maps = [], []
            for i in range(NCORES):
                b, j = i // 2, i % 2
                cols = np.concatenate([np.arange(g_ * 512 + j * 256, g_ * 512 + (j + 1) * 256) for g_ in range(3)])
                wocols = np.concatenate([np.arange(dr * 1024 + o_ * 512 + j * 256, dr * 1024 + o_ * 512 + (j + 1) * 256) for dr in range(2) for o_ in range(2)])
                m = dict(cw=_ca(cw_[:, cols]), cb=_ca(cb_[cols][None]), fw1=f(hy_w1[e]), fw2=f(hy_w2[e]), fw3=f(hy_w3[e]), fb=fbv,
                         fwo=_ca(f(hy_w_out[e])[:, wocols]), skip=_ca(f(hy_skip[e])[:, j * 256:(j + 1) * 256]), m0=misc["m0"])
                for s, src, L in (("l", p, 4096), ("c", pc, 256)):
                    ph = np.zeros((L + 2, 768), np.float32)
                    ph[1:L + 1] = src[b][:, cols]
                    m.update({"phy_" + s: ph, "feats_" + s: K[s]["feats"], "decay_" + s: _ca(K[s]["decay"][:, j * 256:(j + 1) * 256]),
                              "Wc_" + s: K[s]["Wc"], "Ws_" + s: K[s]["Ws"], "Vc_" + s: K[s]["Vc"], "Vs_" + s: K[s]["Vs"]})
                hy_maps.append(m)
                att_maps.append(dict(
                    q=_ca(p[b][:, 1536 + j * 256:1536 + (j + 1) * 256]), k=_ca(p[b][:, 2048 + j * 64:2048 + (j + 1) * 64]), v=_ca(p[b][:, 2176 + j * 64:2176 + (j + 1) * 64]),
                    qc=_ca(pc[b][:, 1536 + j * 256:1536 + (j + 1) * 256]), kc=_ca(pc[b][:, 2048 + j * 64:2048 + (j + 1) * 64]), vc=_ca(pc[b][:, 2176 + j * 64:2176 + (j + 1) * 64]),
                    qg=_ca(f(q_norm_g[e])[None]), kg=_ca(f(k_norm_g[e])[None]), sink=_ca(f(attn_sink[e])[None, j * 4:(j + 1) * 4]),
                    ropeC=misc["ropeC"], ropeS=misc["ropeS"], maskL=misc["maskL"], maskR=misc["maskR"], ident=misc["ident"]))
            rh = _launch("hy", hy_maps)
            ra = _launch("att", att_maps)
            mix = np.zeros((4, 4096, 1024), np.float32)
            mixc = np.zeros((4, 256, 1024), np.float32)
            for i in range(NCORES):
                b, j = i // 2, i % 2
                mix[b][:, j * 256:(j + 1) * 256] = rh[i]["hy_l"]
                mixc[b][:, j * 256:(j + 1) * 256] = rh[i]["hy_c"]
                mix[b][:, 512 + j * 256:512 + (j + 1) * 256] = ra[i]["att"]
                mixc[b][:, 512 + j * 256:512 + (j + 1) * 256] = ra[i]["attc"]
            maps = []
            for i in range(NCORES):
                m = dict(moe_in)
                m.update(xT=xTs[i], yT=_tok_T(mix, mixc, i), cT=cTs[i], wo=f(ev_w_out[e]))
                maps.append(m)
            r = back_half("p3e", maps)
            _scatter_tok("xo", r, xl, xc)
        else:
            o = l // 2
            maps = []
            for i in range(NCORES):
                b, hf = i // 2, i % 2
                xe = np.zeros((Q_T, D), np.float32)
                va = np.zeros((1, Q_T), np.float32)
                for (base, nown), src, t0 in zip(Q_SEGS, (xl[b], xc[b]), (hf * 2048, hf * 128)):
                    lo_, hi_ = max(t0 - 16, 0), min(t0 + nown + 16, src.shape[0])
                    xe[base + (lo_ - (t0 - 16)): base + (hi_ - (t0 - 16))] = src[lo_:hi_]
                    va[0, base + (lo_ - (t0 - 16)): base + (hi_ - (t0 - 16))] = 1.0
                maps.append(dict(xT=_ca(xe.T), valid=va, cT=cTs[i], modw=_ca(mw[:, :2048]), modb=_ca(mb[:2048]), g=f(norm1_g[l]),
                                 w1=f(cf_w1[o]), b1=f(cf_b1[o]), dww=f(cf_dw_w[o]), dwb=f(cf_dw_b[o]), ident=misc["ident"]))
            rq = _launch("q1", maps)
            maps = []
            for i in range(NCORES):
                m = dict(moe_in)
                m.update(xT=_tok_T(xl, xc, i), yT=_ca(rq[i]["cv"]), cT=cTs[i], wo=f(cf_w2[o]), lng=f(cf_ln_g[o]), lnb=f(cf_ln_b[o]), b2=f(cf_b2[o]))
                maps.append(m)
            r = back_half("p3o", maps)
            _scatter_tok("xo", r, xl, xc)
    return xl


ME_TA = 8 * 2176
ME_G = 2176
ME_NE = 4


def build_me(dbg=0):
    P = Prog()
    TA, TG, NX = ME_TA, ME_G, ME_NE
    hTd = P.dram_in("hT", [D, TA], BF16); gwd = P.dram_in("gw", [NX, TA])
    wg = P.dram_in("wg", [NX, D, D]); wu = P.dram_in("wu", [NX, D, D]); wd = P.dram_in("wd", [NX, D, D])
    bg = P.dram_in("bg", [NX, D]); bu = P.dram_in("bu", [NX, D]); bd = P.dram_in("bd", [NX, D])
    ident = P.dram_in("ident", [128, 128])
    yo = P.dram_out("y", [D, TA])
    psG = Rot(P, 3, [128, 512], F32, psum=True, tag="psG")
    psU = Rot(P, 3, [128, 512], F32, psum=True, tag="psU")
    psDn = Rot(P, 2, [128, 512], F32, psum=True, tag="psDn")
    psA = psDn
    idn = P.sb([128, 128]); P.dma(idn[:], ident, writes=["idn"])
    NP_ = 32
    braw = P.sb([NP_, 3, D])
    P.op("pool", lambda e: e.memset(braw[:], 0.0), writes=["braw"])
    P.dma(braw[0:NX, 0, :], bg, writes=["braw"]); P.dma(braw[0:NX, 1, :], bu, writes=["braw"]); P.dma(braw[0:NX, 2, :], bd, writes=["braw"])
    bgT = P.sb([128, KB, NP_]); bg17 = P.sb([128, KB, NP_]); bu1 = P.sb([128, KB, NP_])
    for which, dst in ((0, bgT), (1, bu1)):
        pt, pk = psA.next()
        for fb in range(KB):
            P.op("pe", lambda e, pt=pt, fb=fb, which=which: e.transpose(out=pt[:, fb * NP_:(fb + 1) * NP_], in_=braw[:, which, fb * 128:(fb + 1) * 128], identity=idn[0:NP_, 0:NP_]),
                 reads=["braw", "idn"], writes=[pk], accum=fb > 0)
        P.op("dve", lambda e, pt=pt, dst=dst: e.tensor_copy(out=dst[:].rearrange("p a b -> p (a b)"), in_=pt[:, 0:KB * NP_]), reads=[pk], writes=["bias"])
    P.op("dve", lambda e: e.tensor_scalar(out=bg17[:], in0=bgT[:], scalar1=1.702, scalar2=None, op0=ALU.mult), reads=["bias"], writes=["bias"])
    P.op("dve", lambda e: e.tensor_scalar(out=bu1[:], in0=bu1[:], scalar1=1.0, scalar2=None, op0=ALU.add), reads=["bias"], writes=["bias"])
    bdb = P.sb([NP_, D], BF16)
    P.op("act", lambda e: e.copy(out=bdb[:], in_=braw[:, 2, :]), reads=["braw"], writes=["bdb"])
    hT = P.sb([128, KB, TG], BF16); yacc = P.sb([128, KB, TG])
    gw4 = P.sb([NP_, TG]); gw4b = P.sb([NP_, TG], BF16)
    P.op("pool", lambda e: e.memset(gw4[:], 0.0), writes=["gw4"])
    t512 = Rot(P, 6, [128, 512], F32, tag="t512")
    gwb = Rot(P, 2, [128, TG], F32, tag="gwb")
    aTr = Rot(P, 2, [128, 2, 512], BF16, tag="aT")
    wgr = Rot(P, 3, [128, KB, 256], BF16, tag="wg"); wur = Rot(P, 3, [128, KB, 256], BF16, tag="wu"); wdr = Rot(P, 3, [128, 2, D], BF16, tag="wd")
    for gi in range(TA // TG):
        lo, hi = gi * TG, (gi + 1) * TG
        chs = chunks_of(TG)
        P.dma(hT[:], hTd[:, lo:hi].rearrange("(kb p) t -> p kb t", p=128), writes=[("hT", c0) for c0, _ in chs])
        P.dma(gw4[0:NX, :], gwd[:, lo:hi], writes=["gw4"])
        P.op("act", lambda e: e.copy(out=gw4b[:], in_=gw4[:]), reads=["gw4"], writes=["gw4b"])
        first_unit = True
        if dbg == 1:
            for (c0, n) in chs:
                for db in range(KB):
                    P.op("pool", lambda e, db=db, c0=c0, n=n: e.memset(yacc[:, db, c0:c0 + n], 1.0), writes=[("yacc", c0, db)])
        if dbg in (3, 4, 5, 7, 8, 9):
            for (c0, n) in chs:
                for db in range(KB):
                    P.op("pool", lambda e, db=db, c0=c0, n=n: e.memset(yacc[:, db, c0:c0 + n], 1.0), writes=[("yacc", c0, db)])
        for ex in range(NX if dbg != 1 else 0):
            gb, gbk = gwb.next()
            P.dma(gb[:], gwd[ex:ex + 1, lo:hi].partition_broadcast(128), writes=[gbk])
            if dbg == 3:
                P.op("dve", lambda e, gb=gb: e.tensor_tensor(out=yacc[:, 0, 0:512], in0=yacc[:, 0, 0:512], in1=gb[:, 0:512], op=ALU.add), reads=[gbk, ("yacc", 0, 0)], writes=[("yacc", 0, 0)])
                continue
            for qd in range(4 if dbg not in (2, 4, 5, 7, 8, 9) else 1):
                f0 = qd * 256
                wgt, wgk = wgr.next(); wut, wuk = wur.next(); wdt, wdk = wdr.next()
                P.dma(wgt[:], wg[ex][:, f0:f0 + 256].rearrange("(kb p) f -> p kb f", p=128), writes=[wgk], queue="pool")
                P.dma(wut[:], wu[ex][:, f0:f0 + 256].rearrange("(kb p) f -> p kb f", p=128), writes=[wuk], queue="pool")
                P.dma(wdt[:], wd[ex][f0:f0 + 256, :].rearrange("(fb p) d -> p fb d", p=128), writes=[wdk], queue="pool")
                for (c0, n) in (chs if dbg != 7 else []):
                    at, atk = aTr.next()
                    for fb in range(2):
                        fa = qd * 2 + fb
                        pg, pgk = psG.next(); pu, puk = psU.next()
                        for kb in range(KB if dbg != 9 else 0):
                            P.op("pe", lambda e, pg=pg, wgt=wgt, kb=kb, fb=fb, c0=c0, n=n: e.matmul(pg[:, :n], wgt[:, kb, fb * 128:(fb + 1) * 128], hT[:, kb, c0:c0 + n], start=(kb == 0), stop=(kb == KB - 1)),
                                 reads=[wgk, ("hT", c0)], writes=[pgk], accum=kb > 0)
                        for kb in range(KB if dbg != 9 else 0):
                            P.op("pe", lambda e, pu=pu, wut=wut, kb=kb, fb=fb, c0=c0, n=n: e.matmul(pu[:, :n], wut[:, kb, fb * 128:(fb + 1) * 128], hT[:, kb, c0:c0 + n], start=(kb == 0), stop=(kb == KB - 1)),
                                 reads=[wuk, ("hT", c0)], writes=[puk], accum=kb > 0)
                        if dbg == 8:
                            continue
                        t1, t1k = t512.next(); t2, t2k = t512.next(); t3, t3k = t512.next()
                        P.op("dve", lambda e, t1=t1, pg=pg, fa=fa, ex=ex, n=n: e.tensor_scalar(out=t1[:, :n], in0=pg[:, :n], scalar1=bgT[:, fa, ex:ex + 1], scalar2=7.0, op0=ALU.add, op1=ALU.min),
                             reads=[pgk, "bias"], writes=[t1k])
                        P.op("act", lambda e, t2=t2, t1=t1, n=n: e.activation(out=t2[:, :n], in_=t1[:, :n], func=AF.Sigmoid, scale=1.702),
                             reads=[t1k], writes=[t2k])
                        P.op("act", lambda e, t3=t3, pu=pu, fa=fa, ex=ex, n=n: e.activation(out=t3[:, :n], in_=pu[:, :n], func=AF.Identity, bias=bu1[:, fa, ex:ex + 1], scale=1.0),
                             reads=[puk, "bias"], writes=[t3k])
                        P.op("pool", lambda e, t3=t3, n=n: e.tensor_scalar(out=t3[:, :n], in0=t3[:, :n], scalar1=8.0, scalar2=-6.0, op0=ALU.min, op1=ALU.max),
                             reads=[t3k], writes=[t3k])
                        P.op("dve", lambda e, t2=t2, t1=t1, n=n: e.scalar_tensor_tensor(out=t2[:, :n], in0=t2[:, :n], scalar=SIG_CLAMP, in1=t1[:, :n], op0=ALU.min, op1=ALU.mult),
                             reads=[t1k, t2k], writes=[t2k])
                        P.op("pool", lambda e, t3=t3, t2=t2, n=n: e.tensor_tensor(out=t3[:, :n], in0=t3[:, :n], in1=t2[:, :n], op=ALU.mult),
                             reads=[t2k, t3k], writes=[t3k])
                        P.op("pool", lambda e, at=at, t3=t3, gb=gb, fb=fb, c0=c0, n=n: e.tensor_tensor(out=at[:, fb, :n], in0=t3[:, :n], in1=gb[:, c0:c0 + n], op=ALU.mult),
                             reads=[t3k, gbk], writes=[(atk, fb)])
                    for db in range(KB if dbg not in (4, 8, 9) else 0):
                        pt, pk = psDn.next()
                        for fb in range(2):
                            P.op("pe", lambda e, pt=pt, wdt=wdt, at=at, fb=fb, db=db, n=n, fu=first_unit: e.matmul(pt[:, :n], wdt[:, fb, db * 128:(db + 1) * 128], at[:, fb, :n], start=(fb == 0),
                                                                                                   stop=(fb == 1 and (not fu or dbg == 5))),
                                 reads=[wdk, (atk, fb)], writes=[pk], accum=fb > 0)
                        if first_unit and dbg != 5:
                            P.op("pe", lambda e, pt=pt, db=db, c0=c0, n=n: e.matmul(pt[:, :n], bdb[:, db * 128:(db + 1) * 128], gw4b[:, c0:c0 + n], start=False, stop=True),
                                 reads=["bdb", "gw4b"], writes=[pk], accum=True)
                        if first_unit:
                            P.op("act", lambda e, pt=pt, db=db, c0=c0, n=n: e.copy(out=yacc[:, db, c0:c0 + n], in_=pt[:, :n]), reads=[pk], writes=[("yacc", c0, db)])
                        else:
                            P.op("dve", lambda e, pt=pt, db=db, c0=c0, n=n: e.tensor_tensor(out=yacc[:, db, c0:c0 + n], in0=pt[:, :n], in1=yacc[:, db, c0:c0 + n], op=ALU.add),
                                 reads=[pk, ("yacc", c0, db)], writes=[("yacc", c0, db)])
                first_unit = False
        P.dma(yo[:, lo:hi].rearrange("(kb p) t -> p kb t", p=128), yacc[:], reads=[("yacc", c0, db) for c0, _ in chs for db in range(KB)], writes=[("yo", gi)])
    return P.finish()


def build_p3c(T):
    P = Prog()
    x1 = P.dram_in("x1", [D, T]); yp = P.dram_in("yp", [NCORES, D, T]); cT = P.dram_in("cT", [D, 2])
    modw = P.dram_in("modw", [D, 1024]); modb = P.dram_in("modb", [1024])
    xo = P.dram_out("xo", [D, T])
    psA = Rot(P, 2, [128, 512], F32, psum=True, tag="psA")
    R16 = Rot(P, 4, [128, KB, 512], F32, tag="R16")
    mv, mvk = adaln_vecs(P, cT, modw, modb, 1, psA, "ad", wbuf=R16)
    accr = Rot(P, 2, [128, KB, 512], F32, tag="acc")
    for (c0, n) in split_chunks(0, T, 2048):
        kind = 1 if c0 >= 2048 else 0
        acc, acck = accr.next()
        P.dma(acc[:, :, :n], yp[0][:, c0:c0 + n].rearrange("(kb p) t -> p kb t", p=128), writes=[acck])
        for i in range(1, NCORES):
            yt, ytk = R16.next()
            P.dma(yt[:, :, :n], yp[i][:, c0:c0 + n].rearrange("(kb p) t -> p kb t", p=128), writes=[ytk])
            P.op("dve" if i % 2 else "pool", lambda e, acc=acc, yt=yt, n=n: e.tensor_tensor(out=acc[:, :, :n], in0=acc[:, :, :n], in1=yt[:, :, :n], op=ALU.add),
                 reads=[acck, ytk], writes=[acck])
        xt, xtk = R16.next()
        P.dma(xt[:, :, :n], x1[:, c0:c0 + n].rearrange("(kb p) t -> p kb t", p=128), writes=[xtk])
        for kb in range(KB):
            P.op("dve", lambda e, acc=acc, xt=xt, kb=kb, n=n, kind=kind: e.scalar_tensor_tensor(out=xt[:, kb, :n], in0=acc[:, kb, :n], scalar=mv[:, kb, kind:kind + 1], in1=xt[:, kb, :n],
                                                                                      op0=ALU.mult, op1=ALU.add), reads=[acck, xtk, mvk], writes=[xtk])
        P.dma(xo[:, c0:c0 + n].rearrange("(kb p) t -> p kb t", p=128), xt[:, :, :n], reads=[xtk], writes=[("xo", c0)])
    return P.finish()
```
